# Optimizing a Trainium2 kernel written in Bass

```python
import jax, jax.numpy as jnp
from jax import lax
import numpy as np

D_MODEL = 1024
BATCH = 16
SEQ = 256
DEPTH = 4
DEC_BATCH = 8
DEC_SEQ = 2048
PAST_LEN = 512

GRID_W = 64
GLA_H = 4
GLA_DK = 64
GLA_DV = 128
GLA_RANK = 16
GLA_GATE_NORM = 16.0
HG_H = 4
HG_DK = 128
HG_DV = 128
GM_G = 4
GM_CH = 128
GM_CHUNK = 128
SCAN_CHUNK = 64
D_FF = 4 * D_MODEL
EPS = 1e-6
GLA_QK = GLA_H * GLA_DK
GLA_V = GLA_H * GLA_DV
HG_K = HG_H * HG_DK
HG_V = HG_H * HG_DV
GM_W = GM_G * GM_CH
SPLIT_SIZES = (GLA_QK, GLA_QK, GLA_V, GLA_V, GLA_RANK, GLA_RANK,
               HG_K, HG_K, HG_K, HG_V, HG_V,
               GM_W, GM_W,
               D_MODEL, D_MODEL, D_MODEL)
SPLIT_POINTS = tuple(int(s) for s in np.cumsum(SPLIT_SIZES)[:-1])
D_IN = int(sum(SPLIT_SIZES))

kernel_name = "hybrid_gla_hgrn2_gmlp_diffusion_step"

F32 = jnp.float32


def rms_norm(x, g):
    xf = x.astype(F32)
    xf = xf * lax.rsqrt(jnp.mean(xf * xf, axis=-1, keepdims=True) + EPS)
    return xf.astype(x.dtype) * g


def split_heads(t, n):
    b, s, _ = t.shape
    return t.reshape(b, s, n, -1).transpose(0, 2, 1, 3)


def head_norm_gate(o, g, gate):
    b, h, s, dv = o.shape
    of = o.astype(F32)
    of = of * lax.rsqrt(jnp.mean(of * of, axis=-1, keepdims=True) + EPS)
    of = of.transpose(0, 2, 1, 3).reshape(b, s, h * dv).astype(gate.dtype)
    return of * g * jax.nn.silu(gate)


def chunked_gla(q, k, v, log_a, s0):
    b, h, s, _ = q.shape
    n = s // SCAN_CHUNK

    def to_chunks(t):
        return jnp.moveaxis(t.reshape(b, h, n, SCAN_CHUNK, t.shape[-1]), 2, 0)

    causal = jnp.tril(jnp.ones((SCAN_CHUNK, SCAN_CHUNK), dtype=bool))

    def step(st, inp):
        qc, kc, vc, ac = inp
        cum = jnp.cumsum(ac.astype(F32), axis=2)
        rel = cum[:, :, :, None, :] - cum[:, :, None, :, :]
        decay = jnp.exp(jnp.where(causal[:, :, None], rel, -jnp.inf))
        scores = jnp.einsum('bhid,bhjd,bhijd->bhij', qc, kc, decay)
        o = (jnp.einsum('bhij,bhjv->bhiv', scores, vc)
             + jnp.einsum('bhid,bhdv->bhiv', qc * jnp.exp(cum), st))
        last = cum[:, :, -1:, :]
        st_new = (jnp.exp(last[:, :, 0, :])[..., None] * st
                  + jnp.einsum('bhjd,bhjv->bhdv', kc * jnp.exp(last - cum), vc))
        return st_new.astype(st.dtype), o.astype(v.dtype)

    s_fin, o = lax.scan(step, s0, (to_chunks(q), to_chunks(k), to_chunks(v), to_chunks(log_a)))
    o = jnp.moveaxis(o, 0, 2).reshape(b, h, s, v.shape[-1])
    return o, s_fin


def bidir_scan(q, k_f, k_b, v, la_f, la_b, s0_f, s0_b):
    o_f, s_f = chunked_gla(q, k_f, v, la_f, s0_f)
    flip = lambda t: jnp.flip(t, axis=2)
    o_b, s_b = chunked_gla(flip(q), flip(k_b), flip(v), flip(la_b), s0_b)
    return o_f + flip(o_b), jnp.stack([s_f, s_b], axis=1)


def hgrn_lower_bounds(logits):
    p = jax.nn.softmax(logits.astype(F32), axis=1)
    cum = jnp.cumsum(p, axis=1)
    return cum - cum[:, :1]


def chunk_spatial_gate(u, v, g, ws, bs):
    b, s, _ = u.shape
    n = s // GM_CHUNK
    vf = v.astype(F32)
    mu = jnp.mean(vf, axis=-1, keepdims=True)
    var = jnp.mean(jnp.square(vf - mu), axis=-1, keepdims=True)
    vn = ((vf - mu) * lax.rsqrt(var + EPS)).astype(v.dtype) * g
    vc = vn.reshape(b, n, GM_CHUNK, GM_G, GM_CH)
    mixed = jnp.einsum('gpq,bnqgc->bnpgc', ws, vc) + bs.T[:, :, None]
    return u * mixed.reshape(b, s, GM_W)


def grid_posemb(n_tok, dtype):
    rows = n_tok // GRID_W
    r, col = jnp.meshgrid(jnp.arange(rows, dtype=F32), jnp.arange(GRID_W, dtype=F32), indexing='ij')
    r = r.reshape(-1)
    col = col.reshape(-1)
    nf = D_MODEL // 4
    omega = 1.0 / (10000.0 ** (jnp.arange(nf, dtype=F32) / nf))
    er = r[:, None] * omega
    ec = col[:, None] * omega
    return jnp.concatenate([jnp.sin(er), jnp.cos(er), jnp.sin(ec), jnp.cos(ec)], axis=-1).astype(dtype)


def token_mixers(h, s_gla, s_hg, p, lb_f, lb_b):
    proj = h @ p['w_in']
    (gq, gk, gv, go, gzf, gzb, hq, hff, hfb, hi, ho, mu, mv, a_gla, a_hg, a_gm) = jnp.split(
        proj, SPLIT_POINTS, axis=-1)
    q = split_heads(gq, GLA_H) * (GLA_DK ** -0.5)
    k = split_heads(gk, GLA_H)
    v = split_heads(gv, GLA_H)
    la_f = split_heads(jax.nn.log_sigmoid((gzf @ p['gla_lr_w'][0] + p['gla_lr_b'][0]).astype(F32)) / GLA_GATE_NORM, GLA_H)
    la_b = split_heads(jax.nn.log_sigmoid((gzb @ p['gla_lr_w'][1] + p['gla_lr_b'][1]).astype(F32)) / GLA_GATE_NORM, GLA_H)
    o, s_gla_new = bidir_scan(q, k, k, v, la_f, la_b, s_gla[:, 0], s_gla[:, 1])
    o_gla = head_norm_gate(o, p['gla_norm_g'], go)
    q = split_heads(hq, HG_H) * (HG_DK ** -0.5)
    hff32 = hff.astype(F32)
    hfb32 = hfb.astype(F32)
    lf_f = jnp.log(lb_f + (1.0 - lb_f) * jax.nn.sigmoid(hff32))
    lf_b = jnp.log(lb_b + (1.0 - lb_b) * jax.nn.sigmoid(hfb32))
    k_f = (1.0 - lb_f) * jax.nn.sigmoid(-hff32)
    k_b = (1.0 - lb_b) * jax.nn.sigmoid(-hfb32)
    v = split_heads(hi, HG_H)
    o, s_hg_new = bidir_scan(q, split_heads(k_f, HG_H), split_heads(k_b, HG_H), v,
                             split_heads(lf_f, HG_H), split_heads(lf_b, HG_H), s_hg[:, 0], s_hg[:, 1])
    o_hg = head_norm_gate(o, p['hg_norm_g'], ho)
    o_gm = chunk_spatial_gate(jax.nn.gelu(mu), jax.nn.gelu(mv), p['gm_norm_g'], p['gm_ws'], p['gm_bs'])
    merged = (jax.nn.sigmoid(a_gla) * (o_gla @ p['w_br_gla'])
              + jax.nn.sigmoid(a_hg) * (o_hg @ p['w_br_hg'])
              + jax.nn.sigmoid(a_gm) * (o_gm @ p['w_br_gm']))
    return merged @ p['w_out'], s_gla_new, s_hg_new


def trunk_layer(x, mod, s_gla, s_hg, lb_f, lb_b, p):
    sh1, sc1, gt1, sh2, sc2, gt2 = jnp.split(mod, 6, axis=-1)
    h = rms_norm(x, p['norm_mix_g']) * (1.0 + sc1) + sh1
    mix, s_gla_new, s_hg_new = token_mixers(h, s_gla, s_hg, p, lb_f, lb_b)
    x = x + gt1 * mix
    h = rms_norm(x, p['norm_ffn_g']) * (1.0 + sc2) + sh2
    x = x + gt2 * (jnp.square(jax.nn.relu(h @ p['w_ff1'])) @ p['w_ff2'])
    return x, s_gla_new, s_hg_new


def setup_inputs(seed: int = 0) -> dict:
    key = jax.random.key(seed)
    ks = jax.random.split(key, 26)
    nrm = lambda k, shape, scale: jax.random.normal(k, shape, F32) * scale
    gain = lambda k, shape: 1.0 + 0.02 * jax.random.normal(k, shape, F32)
    D = D_MODEL
    return {
        'x_prompt': nrm(ks[0], (BATCH, SEQ, D), 1.0),
        'x_sample': nrm(ks[1], (DEC_BATCH, DEC_SEQ, D), 1.0),
        'c': nrm(ks[2], (DEC_BATCH, D), 1.0),
        'state_gla': nrm(ks[3], (DEC_BATCH, DEPTH, 2, GLA_H, GLA_DK, GLA_DV), 0.3),
        'state_hgrn': nrm(ks[4], (DEC_BATCH, DEPTH, 2, HG_H, HG_DK, HG_DV), 0.3),
        'c_ctx': nrm(ks[5], (D,), 1.0),
        'ada_w': nrm(ks[6], (DEPTH, D, 6 * D), 0.5 * D ** -0.5),
        'ada_b': nrm(ks[7], (DEPTH, 6 * D), 0.02),
        'norm_mix_g': gain(ks[8], (DEPTH, D)),
        'norm_ffn_g': gain(ks[9], (DEPTH, D)),
        'w_in': nrm(ks[10], (DEPTH, D, D_IN), D ** -0.5),
        'gla_lr_w': nrm(ks[11], (DEPTH, 2, GLA_RANK, GLA_QK), GLA_RANK ** -0.5),
        'gla_lr_b': nrm(ks[12], (DEPTH, 2, GLA_QK), 0.1),
        'gla_norm_g': gain(ks[13], (DEPTH, GLA_V)),
        'hg_lb_logits': nrm(ks[14], (2, DEPTH, HG_K), 0.1),
        'hg_norm_g': gain(ks[15], (DEPTH, HG_V)),
        'gm_norm_g': gain(ks[16], (DEPTH, GM_W)),
        'gm_ws': nrm(ks[17], (DEPTH, GM_G, GM_CHUNK, GM_CHUNK), GM_CHUNK ** -0.5),
        'gm_bs': gain(ks[18], (DEPTH, GM_G, GM_CHUNK)),
        'w_br_gla': nrm(ks[19], (DEPTH, GLA_V, D), GLA_V ** -0.5),
        'w_br_hg': nrm(ks[20], (DEPTH, HG_V, D), HG_V ** -0.5),
        'w_br_gm': nrm(ks[21], (DEPTH, GM_W, D), GM_W ** -0.5),
        'w_out': nrm(ks[22], (DEPTH, D, D), D ** -0.5),
        'w_ff1': nrm(ks[23], (DEPTH, D, D_FF), D ** -0.5),
        'w_ff2': nrm(ks[24], (DEPTH, D_FF, D), D_FF ** -0.5),
        'final_norm_g': gain(ks[25], (D,)),
    }


def reference(x_prompt, x_sample, c, state_gla, state_hgrn, c_ctx, ada_w, ada_b, norm_mix_g, norm_ffn_g,
              w_in, gla_lr_w, gla_lr_b, gla_norm_g, hg_lb_logits, hg_norm_g, gm_norm_g, gm_ws, gm_bs,
              w_br_gla, w_br_hg, w_br_gm, w_out, w_ff1, w_ff2, final_norm_g):
    lbs = hgrn_lower_bounds(hg_lb_logits)
    b_ctx = x_prompt.shape[0]
    xc = x_prompt
    xl = x_sample + grid_posemb(x_sample.shape[1], x_sample.dtype)[None]
    zero_gla = jnp.zeros((b_ctx, 2, GLA_H, GLA_DK, GLA_DV), x_prompt.dtype)
    zero_hg = jnp.zeros((b_ctx, 2, HG_H, HG_DK, HG_DV), x_prompt.dtype)
    new_gla = []
    new_hg = []
    for l in range(DEPTH):
        p = {
            'norm_mix_g': norm_mix_g[l], 'norm_ffn_g': norm_ffn_g[l], 'w_in': w_in[l],
            'gla_lr_w': gla_lr_w[l], 'gla_lr_b': gla_lr_b[l], 'gla_norm_g': gla_norm_g[l],
            'hg_norm_g': hg_norm_g[l], 'gm_norm_g': gm_norm_g[l], 'gm_ws': gm_ws[l], 'gm_bs': gm_bs[l],
            'w_br_gla': w_br_gla[l], 'w_br_hg': w_br_hg[l], 'w_br_gm': w_br_gm[l], 'w_out': w_out[l],
            'w_ff1': w_ff1[l], 'w_ff2': w_ff2[l],
        }
        mod_c = (jax.nn.silu(c_ctx) @ ada_w[l] + ada_b[l])[None, None, :]
        mod_l = (jax.nn.silu(c) @ ada_w[l] + ada_b[l])[:, None, :]
        xc, sg, sh = trunk_layer(xc, mod_c, zero_gla, zero_hg, lbs[0, l], lbs[1, l], p)
        new_gla.append(sg)
        new_hg.append(sh)
        xl, _, _ = trunk_layer(xl, mod_l, state_gla[:, l], state_hgrn[:, l], lbs[0, l], lbs[1, l], p)
    y_prompt = rms_norm(xc, final_norm_g)
    y_sample = rms_norm(xl, final_norm_g)
    new_state_gla = jnp.stack(new_gla, axis=1)
    new_state_hgrn = jnp.stack(new_hg, axis=1)
    return (y_prompt, y_sample, new_state_gla, new_state_hgrn)
```

```python
import math
import numpy as np
import concourse.bass as bass
import concourse.mybir as mybir
from concourse.bass_utils import run_bass_kernel_spmd

F32 = mybir.dt.float32
BF16 = mybir.dt.bfloat16
I32 = mybir.dt.int32
AF = mybir.ActivationFunctionType
ALU = mybir.AluOpType

D = 1024
KC = 8
DEPTH = 4
T_S = 2048
T_P = 512
EPS = 1e-6
NW = 3
A0_PAIR = 2
SLOT_E = 4096
ARENA_W = 23808

O_GQ, O_GK, O_GV, O_GO, O_GZF, O_GZB = 0, 256, 512, 1024, 1536, 1552
O_HQ, O_HFF, O_HFB, O_HI, O_HO = 1568, 2080, 2592, 3104, 3616
O_MU, O_MV, O_AGLA, O_AHG, O_AGM = 4128, 4640, 5152, 6176, 7200

PRM = {}
_o = 0
for _n, _w in [("g1", DEPTH * 8), ("g2", DEPTH * 8), ("adab", DEPTH * 48), ("lrb", DEPTH * 4), ("glag", DEPTH * 4),
               ("hgg", DEPTH * 4), ("gfin", 8), ("cc", 16), ("lg", 32)]:
    PRM[_n] = (_o, _w)
    _o += _w
NPRM = _o


def _fm(v):
    return np.ascontiguousarray(v.reshape(-1, 128).T)


def pack_prm(inp, core):
    a = np.zeros((128, NPRM), np.float32)

    def put(name, arr):
        o, w = PRM[name]
        assert arr.shape == (128, w), (name, arr.shape, w)
        a[:, o:o + w] = arr

    put("g1", np.concatenate([_fm(inp["norm_mix_g"][l]) for l in range(DEPTH)], 1))
    put("g2", np.concatenate([_fm(inp["norm_ffn_g"][l]) for l in range(DEPTH)], 1))
    put("adab", np.concatenate([_fm(inp["ada_b"][l]) for l in range(DEPTH)], 1))
    put("lrb", np.concatenate([_fm(inp["gla_lr_b"][l, d]) for l in range(DEPTH) for d in range(2)], 1))
    put("glag", np.concatenate([_fm(inp["gla_norm_g"][l]) for l in range(DEPTH)], 1))
    put("hgg", np.concatenate([_fm(inp["hg_norm_g"][l]) for l in range(DEPTH)], 1))
    put("gfin", _fm(inp["final_norm_g"]))
    cc = np.stack([_fm(inp["c"][core]), _fm(inp["c_ctx"])], -1).reshape(128, 16)
    put("cc", cc)
    lg = inp["hg_lb_logits"].reshape(2, DEPTH, 4, 128).transpose(3, 0, 2, 1).reshape(128, 32)
    put("lg", lg)
    return a


def _pk(M):
    kc = M.shape[0] // 128
    return M.reshape(kc, 128, M.shape[1]).transpose(1, 0, 2).reshape(128, kc * M.shape[1])


def layer_slices(inp, l):
    w = inp["w_in"][l]
    s = {}
    for i in range(12):
        s["ada%d" % i] = _pk(inp["ada_w"][l][:, i * 512:(i + 1) * 512])
    s["gz"] = _pk(w[:, O_GZF:O_GZF + 32])
    for u in range(2):
        s["gA%d" % u] = _pk(np.concatenate([w[:, O_GQ + u * 128:O_GQ + (u + 1) * 128], w[:, O_GK + u * 128:O_GK + (u + 1) * 128],
                                            w[:, O_GV + u * 256:O_GV + (u + 1) * 256]], 1))
        s["gB%d" % u] = _pk(w[:, O_GO + u * 256:O_GO + (u + 1) * 256])
    for u in range(4):
        sl = lambda o: w[:, o + u * 128:o + (u + 1) * 128]
        s["hA%d" % u] = _pk(np.concatenate([sl(O_HQ), sl(O_HFF), sl(O_HFB), sl(O_HI)], 1))
        s["hB%d" % u] = _pk(sl(O_HO))
    for m, (wb, oa) in enumerate([("w_br_gla", O_AGLA), ("w_br_hg", O_AHG), ("w_br_gm", O_AGM)]):
        s["br%d" % m] = _pk(inp[wb][l])
        s["a%d_0" % m] = _pk(w[:, oa:oa + 512])
        s["a%d_1" % m] = _pk(w[:, oa + 512:oa + 1024])
    s["wo_0"] = _pk(inp["w_out"][l][:, 0:512])
    s["wo_1"] = _pk(inp["w_out"][l][:, 512:1024])
    s["mu"] = _pk(w[:, O_MU:O_MU + 512])
    s["mv"] = _pk(w[:, O_MV:O_MV + 512])
    for f in range(8):
        s["w1_%d" % f] = _pk(inp["w_ff1"][l][:, f * 512:(f + 1) * 512])
        s["w2_%d" % f] = _pk(inp["w_ff2"][l][f * 512:(f + 1) * 512, :])
    return s


SLICE_W = {"gz": 256}
for _i in range(12):
    SLICE_W["ada%d" % _i] = 4096
for _u in range(2):
    SLICE_W["gA%d" % _u] = 4096
    SLICE_W["gB%d" % _u] = 2048
for _u in range(4):
    SLICE_W["hA%d" % _u] = 4096
    SLICE_W["hB%d" % _u] = 1024
for _m in range(3):
    SLICE_W["br%d" % _m] = 4096
    SLICE_W["a%d_0" % _m] = 4096
    SLICE_W["a%d_1" % _m] = 4096
SLICE_W["wo_0"] = 4096
SLICE_W["wo_1"] = 4096
SLICE_W["mu"] = 4096
SLICE_W["mv"] = 4096
for _f in range(8):
    SLICE_W["w1_%d" % _f] = 4096
    SLICE_W["w2_%d" % _f] = 4096
SLICE_ORDER = list(SLICE_W.keys())
SLICE_OFF = {}
_o = 0
for _n in SLICE_ORDER:
    SLICE_OFF[_n] = _o
    _o += SLICE_W[_n]
WL = _o


class Res:
    __slots__ = ("name", "t", "last_w", "readers", "lo", "hi")

    def __init__(self, name, t, lo=0, hi=0):
        self.name = name
        self.t = t
        self.last_w = None
        self.readers = {}
        self.lo = lo
        self.hi = hi


class Ctx:
    def __init__(self, nc):
        self.nc = nc
        self.eng = {"pe": nc.tensor, "act": nc.scalar, "dve": nc.vector, "pool": nc.gpsimd, "sp": nc.sync}
        self.sems = {}
        self.cnt = {}
        self.seen = {e: {} for e in self.eng}
        for e in ("pe", "act", "dve", "pool"):
            self.sems[e] = nc.alloc_semaphore("s_" + e)
            self.cnt[e] = 0
        self.pending = {e: False for e in self.eng}
        self.n_wait = 0
        self.n_ops = 0
        self.live = []
        self.dpool = {"sp": [self.dsem("sp%d" % i) for i in range(8)], "pool": [self.dsem("pq%d" % i) for i in range(4)]}
        self.dpi = {"sp": 0, "pool": 0}

    def sb(self, name, shape, dt):
        return Res(name, self.nc.alloc_sbuf_tensor("sb_" + name, list(shape), dt))

    def ps(self, name, shape=(128, 512), dt=F32):
        return Res(name, self.nc.alloc_psum_tensor("ps_" + name, list(shape), dt))

    def dsem(self, name):
        self.sems[name] = self.nc.alloc_semaphore("d_" + name)
        self.cnt[name] = 0
        return name

    def carve(self, name, arena, lo, hi):
        r = Res(name, arena.t, lo, hi)
        keep = []
        for o in self.live:
            if o.lo < hi and lo < o.hi:
                for k, v in o.readers.items():
                    if r.readers.get(k, 0) < v:
                        r.readers[k] = v
                if o.last_w is not None:
                    k, v = o.last_w
                    if r.readers.get(k, 0) < v:
                        r.readers[k] = v
                if not (o.lo >= lo and o.hi <= hi):
                    keep.append(o)
            else:
                keep.append(o)
        keep.append(r)
        self.live = keep
        return r

    def _need(self, e, reads, writes):
        need = {}

        def add(tok, raw):
            if tok is None:
                return
            k, v = tok
            if k == e and e == "pe":
                return
            if need.get(k, 0) < v:
                need[k] = v

        for r in reads:
            add(r.last_w, True)
        for w in writes:
            add(w.last_w, False)
            for k, v in w.readers.items():
                add((k, v), False)
        return need

    def _emit_waits(self, e, need):
        seen = self.seen[e]
        for k, v in need.items():
            if seen.get(k, 0) < v:
                self.eng[e].wait_ge(self.sems[k], v)
                seen[k] = v
                self.n_wait += 1

    def op(self, e, fn, reads=(), writes=(), signal=True, guards=()):
        need = self._need(e, reads, list(writes) + list(guards))
        self._emit_waits(e, need)
        inst = fn(self.eng[e])
        self.n_ops += 1
        if signal:
            self.cnt[e] += 1
            inst.then_inc(self.sems[e], 1)
            tok = (e, self.cnt[e])
            self.pending[e] = False
        else:
            tok = (e, self.cnt[e] + 1)
            self.pending[e] = True
        for r in reads:
            if r.readers.get(e, 0) < tok[1]:
                r.readers[e] = tok[1]
        for w in writes:
            w.last_w = tok
            w.readers = {}
        return inst

    def dma(self, e, out, in_, reads=(), writes=(), sem=None, **kw):
        if sem is None:
            pool = self.dpool[e]
            sem = pool[self.dpi[e] % len(pool)]
            self.dpi[e] += 1
        need = self._need(e, reads, writes)
        if self.cnt[sem] > 0 and need.get(sem, 0) < self.cnt[sem]:
            need[sem] = self.cnt[sem]
        self._emit_waits(e, need)
        inst = self.eng[e].dma_start(out=out, in_=in_, **kw)
        self.cnt[sem] += 16
        inst.then_inc(self.sems[sem], 16)
        tok = (sem, self.cnt[sem])
        for r in reads:
            if r.readers.get(sem, 0) < tok[1]:
                r.readers[sem] = tok[1]
        for w in writes:
            w.last_w = tok
            w.readers = {}
        return inst

    def finish(self):
        assert not any(self.pending.values()), self.pending
        e = "sp"
        for k, v in self.cnt.items():
            if v > 0 and self.seen[e].get(k, 0) < v:
                self.eng[e].wait_ge(self.sems[k], v)
                self.seen[e][k] = v


def build_nc(depth=DEPTH, groups=("P", "S")):
    nc = bass.Bass("TRN2", target_bir_lowering=False, dynamic_dma_scratch_size=4096)
    C = Ctx(nc)
    dram = lambda n, sh, kind="ExternalInput": nc.dram_tensor(n, list(sh), F32, kind=kind).ap()
    d_xs = dram("xs", [128, KC, T_S])
    d_xp = dram("xp", [128, KC, T_P])
    d_prm = dram("prm", [128, NPRM])
    d_w = dram("wst", [DEPTH, 128, WL])
    d_lrw = dram("lrw", [16, DEPTH * 2 * 256])
    d_gmg = dram("gmg", [DEPTH, 512])
    d_gmb = dram("gmb", [DEPTH, 512])
    d_wsT = dram("wsT", [DEPTH, 128, 512])
    d_sg = dram("sgla", [DEPTH, 2, 2, 128, 128])
    d_sh = dram("shg", [DEPTH, 2, 4, 128, 128])
    d_ys = dram("ys", [128, KC, T_S], "ExternalOutput")
    d_yp = dram("yp", [128, KC, T_P], "ExternalOutput")
    d_ng = dram("ngla", [2, DEPTH, 2, 2, 128, 128], "ExternalOutput")
    d_nh = dram("nhg", [2, DEPTH, 2, 4, 128, 128], "ExternalOutput")

    X = C.sb("X", [128, KC, T_S], F32)
    H = C.sb("H", [128, KC, T_S], BF16)
    XT = [Res("X%d" % i, X.t) for i in range(T_S // 512)]
    HT = [Res("H%d" % i, H.t) for i in range(T_S // 512)]
    WR = [C.sb("wr%d" % i, [128, SLOT_E], BF16) for i in range(NW)]
    P_ = C.sb("prm", [128, NPRM], F32)
    MOD = C.sb("mod", [128, DEPTH, 48, 2], F32)
    DER = C.sb("der", [128, DEPTH, 3, 8, 2], F32)
    LBT = C.sb("lbt", [128, 4, 2, 4, 4], F32)
    IDB = C.sb("idb", [128, 128], BF16)
    ONB = C.sb("onb", [128, 128], BF16)
    ONF = C.sb("onf", [128, 2], F32)
    MSK = C.sb("msk", [128, 768], BF16)
    RM = C.sb("rm", [128, 2], F32)
    ARENA = C.sb("arena", [128, ARENA_W], F32)
    BK = [C.ps("bk%d" % i) for i in range(6)]
    BTS = [C.ps("bt%d" % i, (128, 1024), BF16) for i in range(2)]

    WSEM = [C.dsem("w%d" % i) for i in range(NW)]
    GZW = C.sb("gzw", [128, 256], BF16)

    def prm(name, j):
        o, w = PRM[name]
        return P_.t[:, o + j:o + j + 1]

    class Ar:
        def __init__(self):
            self.cur = 0
            self.limit = ARENA_W

        def reset(self, to=0):
            self.cur = to

        def get(self, name, nbytes):
            nw = (nbytes + 3) // 4
            nw = (nw + 7) // 8 * 8
            lo = self.cur
            self.cur += nw
            assert self.cur <= self.limit, (name, self.cur, self.limit)
            return C.carve(name, ARENA, lo, lo + nw)

    AR = Ar()

    def vf(r, n=None):
        n = (r.hi - r.lo) if n is None else n
        return r.t[:, r.lo:r.lo + n]

    def vb(r, n=None):
        a = r.t[:, r.lo:r.hi].bitcast(BF16)
        return a if n is None else a[:, 0:n]

    rot = {"i": 0, "banks": list(range(6))}

    def bank():
        b = BK[rot["banks"][rot["i"] % len(rot["banks"])]]
        rot["i"] += 1
        assert b.last_w is None or b.last_w[0] != "pe" or b.readers, ("PSUM bank handed out before its previous result was read", b.name)
        return b

    def mm(ps, out, lhsT, rhs, reads, start=True, stop=True, signal=True):
        C.op("pe", lambda e: e.matmul(out, lhsT=lhsT, rhs=rhs, start=start, stop=stop), reads=reads, writes=[ps], signal=signal)

    def act(out, in_, func, reads, writes, scale=None, bias=None):
        kw = {}
        if scale is not None:
            kw["scale"] = scale
        if bias is not None:
            kw["bias"] = bias
        C.op("act", lambda e: e.activation(out=out, in_=in_, func=func, **kw), reads=reads, writes=writes)

    def tt(out, in0, in1, op, reads, writes, eng="dve"):
        C.op(eng, lambda e: e.tensor_tensor(out=out, in0=in0, in1=in1, op=op), reads=reads, writes=writes)

    def ts(out, in0, s1, op0, reads, writes, s2=None, op1=None, eng="dve"):
        if op1 is None:
            C.op(eng, lambda e: e.tensor_scalar(out=out, in0=in0, scalar1=s1, scalar2=None, op0=op0), reads=reads, writes=writes)
        else:
            C.op(eng, lambda e: e.tensor_scalar(out=out, in0=in0, scalar1=s1, scalar2=s2, op0=op0, op1=op1), reads=reads, writes=writes)

    def stt(out, in0, scalar, in1, op0, op1, reads, writes):
        C.op("dve", lambda e: e.scalar_tensor_tensor(out=out, in0=in0, scalar=scalar, in1=in1, op0=op0, op1=op1), reads=reads, writes=writes)

    def recip(out, in_, reads, writes):
        C.op("dve", lambda e: e.reciprocal(out=out, in_=in_), reads=reads, writes=writes)

    NXS = 6
    XS_LO = ARENA_W - NXS * 2048

    class View:
        def __init__(self, ap):
            self.ap = ap

        def __getitem__(self, idx):
            return self.ap[idx]

    XSL = []
    for i in range(NXS):
        lo = XS_LO + i * 2048
        r_ = Res("xs%d" % i, View(ARENA.t[:, lo:lo + 2048].bitcast(BF16)), lo, lo + 2048)
        XSL.append(r_)
    C.live.extend(XSL)
    SLOTS = WR + XSL
    WSEMX = WSEM + [C.dsem("wx%d" % i) for i in range(NXS)]

    class WRing:
        def __init__(self):
            self.seq = []
            self.prev = []
            self.last = {}
            self.issued = 0
            self.cur = 0
            self.released = set()

        def plan(self, items, slots):
            for it in items:
                k = len(self.seq)
                sl = slots[k % len(slots)] if len(slots) == NW else slots[self._rr % len(slots)]
                self._rr += 1
                self.seq.append((it[0], it[1], sl))
                self.prev.append(self.last.get(sl))
                self.last[sl] = k

        _rr = 0

        def _pump(self):
            while self.issued < len(self.seq):
                k = self.issued
                p = self.prev[k]
                if p is not None and p not in self.released:
                    break
                l, name, sl = self.seq[k]
                n = SLICE_W[name]
                o = SLICE_OFF[name]
                b = min(n, 1024)
                slot = SLOTS[sl]
                C.dma("pool", slot.t[:, 0:n].rearrange("p (a b) -> p a b", b=b), d_w[l, :, o:o + n].rearrange("p (a b) -> p a b", b=b),
                      writes=[slot], sem=WSEMX[sl])
                self.issued += 1

        def get(self, l, name):
            k = self.cur
            assert self.seq[k][0:2] == (l, name), (self.seq[k], l, name)
            self._pump()
            assert self.issued > k, ("weight ring stuck", k, name)
            self.cur += 1
            return SLOTS[self.seq[k][2]], k

        def release(self, k):
            self.released.add(k)
            self._pump()

    W = WRing()
    mixer_order = []
    for u in range(2):
        mixer_order += ["gA%d" % u, "gB%d" % u]
    mixer_order += ["br0", "a0_0", "a0_1", "wo_0", "wo_1"]
    for u in range(4):
        mixer_order += ["hA%d" % u, "hB%d" % u]
    mixer_order += ["br1", "a1_0", "a1_1", "wo_0", "wo_1"]
    mixer_order += ["mu", "mv", "br2", "a2_0", "a2_1", "wo_0", "wo_1"]
    for f in range(8):
        mixer_order += ["w1_%d" % f, "w2_%d" % f]
    for l in range(depth):
        W.plan([(l, "ada%d" % i) for i in range(12)], list(range(NW)))
    for g in groups:
        for l in range(depth):
            W.plan([(l, n) for n in mixer_order], list(range(NW + NXS)) if g == "P" else list(range(NW)))

    C.dma("sp", P_.t[:], d_prm, writes=[P_])
    C.op("pool", lambda e: e.memset(ONF.t[:], 1.0), writes=[ONF])
    AR.reset()
    scr = AR.get("scr", 128 * 4)
    C.op("pool", lambda e: e.memset(vf(scr, 128), 0.0), writes=[scr])
    C.op("pool", lambda e: e.affine_select(out=vf(scr, 128), in_=vf(scr, 128), pattern=[[-1, 128]], compare_op=ALU.not_equal,
                                           fill=1.0, base=0, channel_multiplier=1), reads=[scr], writes=[scr])
    C.op("dve", lambda e: e.tensor_copy(out=IDB.t[:], in_=vf(scr, 128)), reads=[scr], writes=[IDB])
    C.op("dve", lambda e: e.memset(ONB.t[:], 1.0), writes=[ONB])
    mscr = AR.get("mscr", 256 * 4)
    mv_ = vf(mscr, 256)
    C.op("pool", lambda e: e.memset(mv_, 0.0), writes=[mscr])
    for d_ in range(2):
        for b_ in range(2):
            reg = ARENA.t[b_ * 64:(b_ + 1) * 64, mscr.lo + d_ * 128 + b_ * 64: mscr.lo + d_ * 128 + (b_ + 1) * 64]
            C.op("pool", lambda e, reg=reg: e.memset(reg, 1.0), reads=[mscr], writes=[mscr])
            pat = [[1, 64]] if d_ == 0 else [[-1, 64]]
            cm = -1 if d_ == 0 else 1
            C.op("pool", lambda e, reg=reg, pat=pat, cm=cm: e.affine_select(out=reg, in_=reg, pattern=pat, compare_op=ALU.is_ge, fill=0.0,
                                                                            base=0, channel_multiplier=cm), reads=[mscr], writes=[mscr])
    C.op("dve", lambda e: e.tensor_copy(out=MSK.t[:, 0:256], in_=mv_), reads=[mscr], writes=[MSK])
    ts(MSK.t[:, 256:512], mv_, float(2.0 ** 58), ALU.mult, [mscr], [MSK])
    C.op("pool", lambda e: e.memset(mv_, 0.0), reads=[mscr], writes=[mscr])
    for d_ in range(2):
        for b_ in range(4):
            reg = ARENA.t[b_ * 32:(b_ + 1) * 32, mscr.lo + d_ * 128 + b_ * 32: mscr.lo + d_ * 128 + (b_ + 1) * 32]
            if b_ == 3:
                continue
            C.op("pool", lambda e, reg=reg: e.memset(reg, 1.0), reads=[mscr], writes=[mscr])
            pat = [[1, 32]] if d_ == 0 else [[-1, 32]]
            cm = -1 if d_ == 0 else 1
            C.op("pool", lambda e, reg=reg, pat=pat, cm=cm: e.affine_select(out=reg, in_=reg, pattern=pat, compare_op=ALU.is_ge, fill=0.0,
                                                                            base=0, channel_multiplier=cm), reads=[mscr], writes=[mscr])
        reg = ARENA.t[64:128, mscr.lo + d_ * 128 + 96: mscr.lo + d_ * 128 + 128]
        C.op("pool", lambda e, reg=reg: e.memset(reg, 1.0), reads=[mscr], writes=[mscr])
        pat = [[1, 32]] if d_ == 0 else [[-1, 32]]
        cm = -1 if d_ == 0 else 1
        C.op("pool", lambda e, reg=reg, pat=pat, cm=cm: e.affine_select(out=reg, in_=reg, pattern=pat, compare_op=ALU.is_ge, fill=0.0,
                                                                        base=(32 if cm == -1 else -32), channel_multiplier=cm), reads=[mscr], writes=[mscr])
        C.op("pool", lambda e, reg=reg: e.affine_select(out=reg, in_=reg, pattern=[[0, 32]], compare_op=ALU.is_ge, fill=0.0,
                                                        base=-32, channel_multiplier=1), reads=[mscr], writes=[mscr])
    ts(MSK.t[:, 512:768], mv_, float(2.0 ** 58), ALU.mult, [mscr], [MSK])
    C.op("pool", lambda e: e.memset(RM.t[:], 1.0), writes=[RM])
    C.op("pool", lambda e: e.affine_select(out=RM.t[:], in_=RM.t[:], pattern=[[0, 2]], compare_op=ALU.is_ge, fill=0.0, base=-96,
                                           channel_multiplier=1), reads=[RM], writes=[RM])

    o_lg = PRM["lg"][0]
    LG = P_.t[:, o_lg:o_lg + 32].rearrange("p (d h l) -> p d h l", d=2, h=4)
    ex = AR.get("lbex", 32 * 4)
    EXv = vf(ex, 32).rearrange("p (d h l) -> p d h l", d=2, h=4)
    sm = AR.get("lbsm", 8 * 4)
    SMv = vf(sm, 8).rearrange("p (d h) -> p d h", d=2)
    act(EXv, LG, AF.Exp, [P_], [ex])
    tt(SMv, EXv[:, :, :, 0], EXv[:, :, :, 1], ALU.add, [ex], [sm])
    tt(SMv, SMv, EXv[:, :, :, 2], ALU.add, [ex, sm], [sm])
    tt(SMv, SMv, EXv[:, :, :, 3], ALU.add, [ex, sm], [sm])
    recip(SMv, SMv, [sm], [sm])
    tt(EXv, EXv, vf(sm, 8).rearrange("p (d h) -> p d h", d=2).unsqueeze(3).broadcast_to([128, 2, 4, 4]), ALU.mult, [ex, sm], [ex])
    C.op("dve", lambda e: e.memset(LBT.t[:, 0, :, :, 0], 0.0), writes=[LBT])
    C.op("dve", lambda e: e.tensor_copy(out=LBT.t[:, 0, :, :, 1], in_=EXv[:, :, :, 1]), reads=[ex], writes=[LBT])
    tt(LBT.t[:, 0, :, :, 2], LBT.t[:, 0, :, :, 1], EXv[:, :, :, 2], ALU.add, [ex, LBT], [LBT])
    tt(LBT.t[:, 0, :, :, 3], LBT.t[:, 0, :, :, 2], EXv[:, :, :, 3], ALU.add, [ex, LBT], [LBT])
    ts(LBT.t[:, 1], LBT.t[:, 0], -1.0, ALU.mult, [LBT], [LBT], s2=1.0, op1=ALU.add)
    ts(LBT.t[:, 2], LBT.t[:, 1], -1.0, ALU.mult, [LBT], [LBT])
    o_lb = PRM["lrb"][0]
    lrb_v = P_.t[:, o_lb:o_lb + DEPTH * 4].rearrange("p (l d u) -> p d u l", d=2, u=2)
    ts(LBT.t[:, 3, :, 0:2, :], lrb_v, -1.0, ALU.mult, [P_], [LBT])

    o_cc = PRM["cc"][0]
    CCv = P_.t[:, o_cc:o_cc + 16]
    sc1 = AR.get("sc1", 16 * 4)
    scb = AR.get("scb", 16 * 2)
    act(vf(sc1, 16), CCv, AF.Exp, [P_], [sc1], scale=-1.0)
    ts(vf(sc1, 16), vf(sc1, 16), 1.0, ALU.add, [sc1], [sc1])
    recip(vf(sc1, 16), vf(sc1, 16), [sc1], [sc1])
    tt(vb(scb, 16), vf(sc1, 16), CCv, ALU.mult, [sc1, P_], [scb])
    SCB = vb(scb, 16).rearrange("p (k t) -> p k t", t=2)
    o_ab = PRM["adab"][0]
    for l in range(depth):
        for s in range(12):
            slot, wk = W.get(l, "ada%d" % s)
            sv = slot.t[:, 0:4096].rearrange("p (k n) -> p k n", n=512)
            ps = bank()
            for oi in range(4):
                for kc in range(KC):
                    mm(ps, ps.t[:, oi * 2:(oi + 1) * 2], sv[:, kc, oi * 128:(oi + 1) * 128], SCB[:, kc, :], [slot, scb],
                       start=(kc == 0), stop=(kc == KC - 1), signal=(kc == KC - 1 and oi == 3))
            ab = P_.t[:, o_ab + l * 48 + s * 4:o_ab + l * 48 + s * 4 + 4].unsqueeze(2).broadcast_to([128, 4, 2])
            tt(MOD.t[:, l, s * 4:(s + 1) * 4, :], ps.t[:, 0:8].rearrange("p (o t) -> p o t", t=2), ab, ALU.add, [ps, P_], [MOD])
            W.release(wk)
        o_g1 = PRM["g1"][0]
        o_g2 = PRM["g2"][0]
        for j, (og, mo) in enumerate([(o_g1, 8), (o_g2, 32)]):
            ts(DER.t[:, l, j], MOD.t[:, l, mo:mo + 8, :], 1.0, ALU.add, [MOD], [DER])
            gv = P_.t[:, og + l * 8:og + l * 8 + 8].unsqueeze(2).broadcast_to([128, 8, 2])
            tt(DER.t[:, l, j], DER.t[:, l, j], gv, ALU.mult, [DER, P_], [DER])
        ts(DER.t[:, l, 2], MOD.t[:, l, 16:24, :], 0.5, ALU.mult, [MOD], [DER])

    def run_group(grp):
        gi = 0 if grp == "S" else 1
        T = T_S if grp == "S" else T_P
        NT = T // 512
        NSC = T // 128
        NCH = T // 64
        seqs = [(0, 2048)] if grp == "S" else [(0, 256), (256, 256)]
        d_x = d_xs if grp == "S" else d_xp
        d_y = d_ys if grp == "S" else d_yp
        tl = lambda t: slice(t * 512, (t + 1) * 512)
        AR.limit = XS_LO if grp == "P" else ARENA_W

        for kc in range(KC):
            C.dma("sp", X.t[:, kc, 0:T], d_x[:, kc, :], writes=XT[0:NT])

        if grp == "S":
            posemb()

        for l in range(depth):
            if l == 0:
                norm_mod(l, gi, T, NT, 0)
            scan_mixer(l, gi, grp, T, NT, NSC, NCH, seqs, "gla")
            finish_mixer(l, gi, T, NT, 0)
            scan_mixer(l, gi, grp, T, NT, NSC, NCH, seqs, "hg")
            finish_mixer(l, gi, T, NT, 1)
            gmlp(l, gi, T, NT)
            finish_mixer(l, gi, T, NT, 2, norm_next=(l, 1))
            ffn(l, gi, T, NT, norm_next=((l + 1, 0) if l + 1 < depth else None))

        AR.reset()
        SQ8 = AR.get("sq8", 8 * 512 * 2)
        YT = [AR.get("yt%d" % i, 8 * 512 * 4) for i in range(2)]
        o_gf = PRM["gfin"][0]
        for t in range(NT):
            rs = rms_rstd(XT[t], lambda kc: X.t[:, kc, tl(t)], SQ8, 1.0 / D)
            y = YT[t % 2]
            yv = vf(y, 4096).rearrange("p (k n) -> p k n", n=512)
            for kc in range(KC):
                stt(yv[:, kc, :], X.t[:, kc, tl(t)], P_.t[:, o_gf + kc:o_gf + kc + 1], rs.t[:], ALU.mult, ALU.mult, [XT[t], P_, rs], [y])
            C.dma("sp", d_y[:, :, tl(t)], yv, reads=[y])

    def rms_rstd(src_res, src_ap, SQ8, inv_n):
        sq = vb(SQ8, 4096).rearrange("p (k n) -> p k n", n=512)
        for kc in range(KC):
            act(sq[:, kc, :], src_ap(kc), AF.Square, [src_res], [SQ8])
        ps = bank()
        for kc in range(KC):
            mm(ps, ps.t[:], ONB.t[:], sq[:, kc, :], [ONB, SQ8], start=(kc == 0), stop=(kc == KC - 1), signal=(kc == KC - 1))
        act(ps.t[:], ps.t[:], AF.Ln, [ps, EPSC], [ps], scale=inv_n, bias=EPSC.t[:, 0:1])
        act(ps.t[:], ps.t[:], AF.Exp, [ps], [ps], scale=-0.5)
        return ps

    nrm_cnt = {"n": 0}

    def norm_alloc():
        SQ8 = AR.get("sq8", 8 * 512 * 2)
        TT = [AR.get("tt%d" % i, 512 * 4) for i in range(3)]
        return SQ8, TT

    def norm_tile(l, gi, which, t, bufs):
        SQ8, TT = bufs
        sh0 = 0 if which == 0 else 24
        cs = slice(t * 512, (t + 1) * 512)
        rs = rms_rstd(XT[t], lambda kc: X.t[:, kc, cs], SQ8, 1.0 / D)
        for kc in range(KC):
            tb = TT[nrm_cnt["n"] % 3]
            nrm_cnt["n"] += 1
            stt(vf(tb, 512), X.t[:, kc, cs], DER.t[:, l, which, kc, gi:gi + 1], rs.t[:], ALU.mult, ALU.mult, [XT[t], DER, rs], [tb])
            act(H.t[:, kc, cs], vf(tb, 512), AF.Identity, [tb, MOD], [HT[t]], bias=MOD.t[:, l, sh0 + kc, gi:gi + 1])

    def norm_mod(l, gi, T, NT, which):
        AR.reset()
        bufs = norm_alloc()
        for t in range(NT):
            norm_tile(l, gi, which, t, bufs)

    def posemb():
        AR.reset()
        T = T_S
        RI = AR.get("pe_r", T * 4)
        CI = AR.get("pe_c", T * 4)
        AG = AR.get("pe_a", T * 4)
        NI = AR.get("pe_n", T * 4)
        NF = AR.get("pe_f", T * 4)
        OMG = AR.get("pe_w", 2 * 4)
        C.op("pool", lambda e: e.iota(vf(RI, T), pattern=[[1, 32], [0, 64]], base=0, channel_multiplier=0,
                                      allow_small_or_imprecise_dtypes=True), writes=[RI])
        C.op("pool", lambda e: e.iota(vf(CI, T), pattern=[[0, 32], [1, 64]], base=0, channel_multiplier=0,
                                      allow_small_or_imprecise_dtypes=True), writes=[CI])
        C.op("pool", lambda e: e.iota(vf(OMG, 2), pattern=[[128, 2]], base=0, channel_multiplier=1,
                                      allow_small_or_imprecise_dtypes=True), writes=[OMG])
        act(vf(OMG, 2), vf(OMG, 2), AF.Exp, [OMG], [OMG], scale=-math.log(10000.0) / 256.0)
        TWO_PI = 2.0 * math.pi
        for kc in range(KC):
            src = RI if kc < 4 else CI
            ph = 0.0 if (kc // 2) % 2 == 0 else math.pi / 2
            ts(vf(AG, T), vf(src, T), vf(OMG, 2)[:, kc % 2:kc % 2 + 1], ALU.mult, [src, OMG], [AG], s2=ph, op1=ALU.add)
            ni = ARENA.t[:, NI.lo:NI.lo + T].bitcast(I32)
            ts(ni, vf(AG, T), 1.0 / TWO_PI, ALU.mult, [AG], [NI])
            C.op("dve", lambda e, ni=ni: e.tensor_copy(out=vf(NF, T), in_=ni), reads=[NI], writes=[NF])
            stt(vf(AG, T), vf(NF, T), -TWO_PI, vf(AG, T), ALU.mult, ALU.add, [NF, AG], [AG])
            ts(vf(NF, T), vf(AG, T), math.pi, ALU.is_gt, [AG], [NF])
            stt(vf(AG, T), vf(NF, T), -TWO_PI, vf(AG, T), ALU.mult, ALU.add, [NF, AG], [AG])
            ts(vf(NF, T), vf(AG, T), -math.pi, ALU.is_lt, [AG], [NF])
            stt(vf(AG, T), vf(NF, T), TWO_PI, vf(AG, T), ALU.mult, ALU.add, [NF, AG], [AG])
            act(vf(NF, T), vf(AG, T), AF.Sin, [AG], [NF])
            tt(X.t[:, kc, :], X.t[:, kc, :], vf(NF, T), ALU.add, XT + [NF], XT)

    def scan_mixer(l, gi, grp, T, NT, NSC, NCH_unused, seqs, mix):
        gla = mix == "gla"
        es = (-1.0 / 16.0) if gla else 1.0
        nun = 2 if gla else 4
        hpu = 2 if gla else 1
        dk = 64 if gla else 128
        CH = 64 if (gla or l > 0) else 32
        NCH = T // CH
        nsub = 128 // CH
        HC = CH // 2
        mk0 = 0 if gla else (256 if CH == 64 else 512)
        AR.reset()
        OM = AR.get("OM", 4 * T * 2)
        base = AR.cur
        rot["banks"] = [0, 1, 2, 3]
        if gla:
            gzs = GZW
            og = SLICE_OFF["gz"]
            C.dma("pool", GZW.t[:, 0:256], d_w[l, :, og:og + 256], writes=[GZW])
        for u in range(nun):
            AR.reset(base)
            RW = max(NCH * 64 + 8, T + 8)
            QO = RW - (T + 8)
            QS = [AR.get("QS%d" % d, RW * 4) for d in range(2)]
            QK = [AR.get(n, T * 2) for n in ("QF", "QB", "KF", "KB")]
            QF, QB, KF, KB = QK
            V = AR.get("V", NSC * hpu * 128 * 2)
            EX = [AR.get("EX%d" % i, 512 * 4) for i in range(4)]
            UR = [AR.get("UR%d" % i, 128 * 4) for i in range(4)]
            KT = [AR.get("KT%d" % i, 128 * 2) for i in range(2)]
            AM = [AR.get("AM%d" % i, 512 * 2) for i in range(2)]
            SQ = AR.get("SQ", 512 * 2)
            OTS = AR.get("OTS", 512 * 4)
            SG = AR.get("SG", 512 * 4)
            CS = AR.get("CS", 2 * 3 * NCH * 4)
            CE = AR.get("CE", 2 * 6 * NCH * 4)
            ST = [[AR.get("ST%d%d" % (d, i), 128 * 4) for i in range(3 if nsub == 2 else 2)] for d in range(2)]
            if gla:
                GZ = [AR.get("GZ%d" % d, 512 * 2) for d in range(4)]
                LRW = AR.get("LRW", 512 * 2)
            KZ = AR.get("KZ", 128 * 2)
            Qv = [ARENA.t[:, q.lo + QO:q.lo + QO + T + 1] for q in QS]
            Sv = [ARENA.t[:, q.lo:q.lo + NCH * 64].bitcast(BF16).rearrange("p (c v) -> p c v", v=128) for q in QS]
            SV = [Res("SV%d" % d, ARENA.t, QS[d].lo, QS[d].hi) for d in range(2)]
            C.live.extend(SV)
            QT = [[Res("QT%d_%d" % (d, t), ARENA.t, QS[d].lo, QS[d].hi) for t in range(NT)] for d in range(2)]
            for d in range(2):
                C.live.extend(QT[d])
                for r_ in QT[d] + [SV[d]]:
                    r_.readers = dict(QS[d].readers)
            CSv = vf(CS, 6 * NCH).rearrange("p (d k c) -> p d k c", d=2, k=3)
            CEv = vf(CE, 12 * NCH).rearrange("p (d k c) -> p d k c", d=2, k=6)
            Vv = vb(V, NSC * hpu * 128).rearrange("p (s n) -> p s n", n=hpu * 128)
            wA, wAk = W.get(l, ("gA%d" if gla else "hA%d") % u)
            wAv = wA.t[:, 0:4096].rearrange("p (k n) -> p k n", n=512)
            wB, wBk = W.get(l, ("gB%d" if gla else "hB%d") % u)
            wBv = wB.t[:, 0:KC * hpu * 128].rearrange("p (k n) -> p k n", n=hpu * 128)
            if gla:
                C.dma("pool", vb(LRW, 512)[0:16, :], d_lrw[:, l * 512:(l + 1) * 512], writes=[LRW])
            if grp == "S":
                for d in range(2):
                    src = d_sg[l, d, u] if gla else d_sh[l, d, u]
                    C.dma("sp", vf(ST[d][0], 128), src, writes=[ST[d][0]])
            def vproj(t):
                for s4 in range(4):
                    sc = t * 4 + s4
                    psv = bank()
                    vcols = slice(256, 512) if gla else slice(384, 512)
                    for kc in range(KC):
                        mm(psv, psv.t[:, 0:hpu * 128], H.t[:, kc, sc * 128:(sc + 1) * 128], wAv[:, kc, vcols], [HT[t], wA],
                           start=(kc == 0), stop=(kc == KC - 1), signal=(kc == KC - 1))
                    C.op("dve", lambda e, sc=sc, psv=psv: e.tensor_copy(out=Vv[:, sc, :], in_=psv.t[:, 0:hpu * 128]), reads=[psv], writes=[V])

            for d in range(2):
                C.op("dve", lambda e, d=d: e.memset(Qv[d][:, 0:1], 0.0), writes=[QT[d][0]], guards=[QS[d]])

            def tile_scan(t):
                for d in range(2):
                    lo = 1 + t * 512
                    C.op("dve", lambda e, d=d, lo=lo: e.tensor_tensor_scan(out=Qv[d][:, lo:lo + 512], data0=ONF.t[:, 0:1].broadcast_to([128, 512]),
                                                                          data1=Qv[d][:, lo:lo + 512], initial=Qv[d][:, lo - 1:lo],
                                                                          op0=ALU.mult, op1=ALU.add),
                         reads=[QT[d][t], QT[d][max(t - 1, 0)], ONF], writes=[QT[d][t]])

            for tp in range(0, NT, A0_PAIR):
                tiles = [t for t in range(tp, tp + A0_PAIR) if t < NT]
                chs = [(t, d) for t in tiles for d in range(2)]
                exo = lambda t, d: EX[(t % 2) * 2 + d]
                csl = lambda t: slice(t * 512, (t + 1) * 512)
                qsl = lambda t: slice(1 + t * 512, 1 + (t + 1) * 512)
                for t in tiles:
                    vproj(t)
                pz = {}
                if gla:
                    gzv = gzs.t[:, 0:256].rearrange("p (k n) -> p k n", n=32)
                    for (t, d) in chs:
                        ps = bank()
                        for kc in range(KC):
                            mm(ps, ps.t[0:16, :], gzv[:, kc, d * 16:(d + 1) * 16], H.t[:, kc, csl(t)], [gzs, HT[t]], start=(kc == 0), stop=(kc == KC - 1),
                               signal=(kc == KC - 1))
                        pz[(t, d)] = ps
                    gzb = lambda t, d: GZ[(t % 2) * 2 + d]
                    for (t, d) in chs:
                        act(vb(gzb(t, d), 512)[0:16, :], pz[(t, d)].t[0:16, :], AF.Copy, [pz[(t, d)]], [gzb(t, d)])
                    pz2 = {}
                    for (t, d) in chs:
                        ps2 = bank()
                        mm(ps2, ps2.t[:], vb(LRW, 512)[0:16, d * 256 + u * 128:d * 256 + (u + 1) * 128], vb(gzb(t, d), 512)[0:16, :], [LRW, gzb(t, d)])
                        pz2[(t, d)] = ps2
                    for (t, d) in chs:
                        e_ = exo(t, d)
                        act(vf(e_, 512), pz2[(t, d)].t[:], AF.Exp, [pz2[(t, d)], LBT], [e_], scale=-1.0, bias=LBT.t[:, 3, d, u, l:l + 1])
                    for (t, d) in chs:
                        e_ = exo(t, d)
                        act(Qv[d][:, qsl(t)], vf(e_, 512), AF.Ln, [e_, ONF], [QT[d][t]], bias=ONF.t[:, 0:1])
                else:
                    for (t, d) in chs:
                        ps = bank()
                        for kc in range(KC):
                            mm(ps, ps.t[:], wAv[:, kc, (1 + d) * 128:(2 + d) * 128], H.t[:, kc, csl(t)], [wA, HT[t]], start=(kc == 0), stop=(kc == KC - 1),
                               signal=(kc == KC - 1))
                        pz[(t, d)] = ps
                    for (t, d) in chs:
                        e_ = exo(t, d)
                        act(vf(e_, 512), pz[(t, d)].t[:], AF.Exp, [pz[(t, d)]], [e_], scale=-1.0)
                    for (t, d) in chs:
                        e_ = exo(t, d)
                        act(vf(e_, 512), vf(e_, 512), AF.Ln, [e_, ONF], [e_], bias=ONF.t[:, 0:1])
                    for (t, d) in chs:
                        e_ = exo(t, d)
                        act(vf(e_, 512), vf(e_, 512), AF.Exp, [e_], [e_], scale=-1.0)
                    for (t, d) in chs:
                        e_ = exo(t, d)
                        act(Qv[d][:, qsl(t)], vf(e_, 512), AF.Ln, [e_, LBT], [QT[d][t]],
                            scale=LBT.t[:, 1, d, u, l:l + 1], bias=LBT.t[:, 0, d, u, l:l + 1])
                    for (t, d) in chs:
                        e_ = exo(t, d)
                        kd = KF if d == 0 else KB
                        ts(vb(kd, T)[:, csl(t)], vf(e_, 512), LBT.t[:, 2, d, u, l:l + 1], ALU.mult, [e_, LBT], [kd],
                           s2=LBT.t[:, 1, d, u, l:l + 1], op1=ALU.add)
                for t in tiles:
                    tile_scan(t)
            for d in range(2):
                q0 = Qv[d][:, 0:T].rearrange("p (c i) -> p c i", i=CH)
                q1 = Qv[d][:, 1:T + 1].rearrange("p (c i) -> p c i", i=CH)
                tt(CSv[:, d, 0, :], q0[:, :, HC], q0[:, :, 0], ALU.subtract, QT[d], [CS])
                tt(CSv[:, d, 1, :], q1[:, :, CH - 1], q0[:, :, HC], ALU.subtract, QT[d], [CS])
                C.op("dve", lambda e, d=d: e.memset(CSv[:, d, 2, NCH - 1:NCH], 0.0), writes=[CS])
                if NCH > 1:
                    tt(CSv[:, d, 2, 0:NCH - 1], q0[:, 1:NCH, HC], q0[:, 0:NCH - 1, HC], ALU.subtract, QT[d], [CS])
            bc = (lambda j: EPSC.t[:, 4:5]) if gla else (lambda j: EPSC.t[:, j:j + 1])
            act(CEv[:, :, 0:2, :], CSv[:, :, 0:2, :], AF.Exp, [CS, EPSC], [CE], scale=es, bias=bc(7))
            act(CEv[:, :, 2:4, :], CSv[:, :, 0:2, :], AF.Exp, [CS, EPSC], [CE], scale=es, bias=bc(5))
            act(CEv[:, :, 4, :], CSv[:, :, 2, :], AF.Exp, [CS], [CE], scale=es)
            act(CEv[:, :, 5, :], CSv[:, :, 2, :], AF.Exp, [CS, EPSC], [CE], scale=es, bias=bc(8))

            zi = [0, 0]

            def chunk_info(d, c):
                t0 = c * CH
                for si, (s0, sl) in enumerate(seqs):
                    if s0 <= t0 < s0 + sl:
                        break
                first = (t0 == s0) if d == 0 else (t0 + CH == s0 + sl)
                last = (t0 + CH == s0 + sl) if d == 0 else (t0 == s0)
                if last:
                    al = CEv[:, d, (3 if d == 0 else 2), c:c + 1]
                    be = CEv[:, d, (1 if d == 0 else 0), c:c + 1]
                else:
                    cj = c if d == 0 else c - 1
                    al = CEv[:, d, 4, cj:cj + 1]
                    be = CEv[:, d, 5, cj:cj + 1]
                return si, first, last, al, be

            cnt = {"kt": 0, "ur": 0, "pu": 0}

            def stage1(d, sc):
                kd = KF if d == 0 else KB
                sub = list(range(nsub)) if d == 0 else list(reversed(range(nsub)))
                ktn = cnt["kt"]
                cnt["kt"] += 1
                ptres = BTS[ktn % 2]
                ptr = ptres.t[:, 0:128]
                C.op("pe", lambda e: e.transpose(out=ptr, in_=vb(kd, T)[:, sc * 128:(sc + 1) * 128], identity=IDB.t[:]),
                     reads=[kd, IDB], writes=[ptres])
                kt = KT[ktn % 2]
                C.op("dve", lambda e: e.tensor_copy(out=vb(kt, 128), in_=ptr), reads=[ptres], writes=[kt])
                if CH == 32:
                    ts(vb(KZ, 128)[64:128, :], ptres.t[64:128, 0:128], RM.t[64:128, 0:1], ALU.mult, [ptres, RM], [KZ])
                items = []
                for r in sub:
                    c = nsub * sc + r
                    pun = cnt["pu"]
                    cnt["pu"] += 1
                    pur = BK[4 + pun % 2]
                    pu = pur.t[:, 0:128]
                    rows = slice(r * CH, (r + 1) * CH)
                    if gla:
                        for hh in range(2):
                            mm(pur, pur.t[hh * 64:(hh + 1) * 64, 0:128],
                               vb(kt, 128)[rows, hh * 64:(hh + 1) * 64], Vv[rows, sc, hh * 128:(hh + 1) * 128], [kt, V], signal=(hh == 1))
                    elif CH == 32 and r == 3:
                        mm(pur, pu, vb(KZ, 128)[64:128, :], Vv[64:128, sc, :], [KZ, V])
                    else:
                        mm(pur, pu, vb(kt, 128)[rows, :], Vv[rows, sc, :], [kt, V])
                    info = chunk_info(d, c)
                    ur = UR[cnt["ur"] % len(UR)]
                    cnt["ur"] += 1
                    act(vf(ur, 128), pu, AF.Identity, [pur, CE], [ur], scale=info[4])
                    items.append((c, ur, info))
                return items

            def stage2(d, items):
                for c, ur, (si, first, last, al, be) in items:
                    zc = ST[d][zi[d]]
                    zn = ST[d][(zi[d] + 1) % len(ST[d])]
                    if first:
                        if grp == "P":
                            C.op("dve", lambda e, zc=zc: e.memset(vf(zc, 128), 0.0), writes=[zc])
                        else:
                            act(vf(zc, 128), vf(zc, 128), AF.Identity, [zc, CE], [zc], scale=CEv[:, d, (0 if d == 0 else 1), c:c + 1])
                    C.op("act", lambda e, zc=zc, c=c: e.activation(out=Sv[d][:, c, :], in_=vf(zc, 128), func=AF.Copy), reads=[zc], writes=[SV[d]],
                         guards=QT[d])
                    stt(vf(zn, 128), vf(zc, 128), al, vf(ur, 128), ALU.mult, ALU.add, [zc, CE, ur], [zn])
                    zi[d] = (zi[d] + 1) % len(ST[d])
                    if last and grp == "P":
                        dst = d_ng[si, l, d, u] if gla else d_nh[si, l, d, u]
                        C.dma("sp", dst, vf(zn, 128), reads=[zn])

            pend = {0: None, 1: None}

            def push(d, sc):
                items = stage1(d, sc)
                if 2 * nsub > len(UR):
                    stage2(d, items)
                    return
                if pend[d] is not None:
                    stage2(d, pend[d])
                pend[d] = items

            def flush(d):
                if pend[d] is not None:
                    stage2(d, pend[d])
                    pend[d] = None

            def factors(t):
                for d in range(2):
                    xa = EX[2 * d]
                    off = 1 if d == 0 else 0
                    qq = Qv[d][:, off + t * 512:off + (t + 1) * 512].rearrange("p (c i) -> p c i", i=CH)
                    ref = Qv[d][:, t * 512:(t + 1) * 512].rearrange("p (c i) -> p c i", i=CH)[:, :, HC:HC + 1].broadcast_to([128, 512 // CH, CH])
                    tt(vf(xa, 512).rearrange("p (c i) -> p c i", i=CH), qq, ref, ALU.subtract, [QT[d][t], QT[d][max(t - 1, 0)]], [xa])
                    if not gla and l == 0:
                        ts(vf(xa, 512), vf(xa, 512), 62.0, ALU.min, [xa], [xa], s2=-62.0, op1=ALU.max)
                for d in range(2):
                    xa, xb = EX[2 * d], EX[2 * d + 1]
                    sq_ = es if d == 0 else -es
                    act(vf(xb, 512), vf(xa, 512), AF.Exp, [xa, EPSC], [xb], scale=-sq_, bias=EPSC.t[:, (4 if gla else 5):(5 if gla else 6)])
                for d in range(2):
                    xa = EX[2 * d]
                    sq_ = es if d == 0 else -es
                    act(vf(xa, 512), vf(xa, 512), AF.Exp, [xa, EPSC], [xa], scale=sq_, bias=EPSC.t[:, (1 if gla else 2):(2 if gla else 3)])

            qkps = {}

            def qk_mm(t):
                cs = slice(t * 512, (t + 1) * 512)
                psq = bank()
                for kc in range(KC):
                    mm(psq, psq.t[:], wAv[:, kc, 0:128], H.t[:, kc, cs], [wA, HT[t]], start=(kc == 0), stop=(kc == KC - 1), signal=(kc == KC - 1))
                psk = None
                if gla:
                    psk = bank()
                    for kc in range(KC):
                        mm(psk, psk.t[:], wAv[:, kc, 128:256], H.t[:, kc, cs], [wA, HT[t]], start=(kc == 0), stop=(kc == KC - 1), signal=(kc == KC - 1))
                qkps[t] = (psq, psk)

            def qk(t):
                cs = slice(t * 512, (t + 1) * 512)
                if t + 1 < NT:
                    qk_mm(t + 1)
                psq, psk = qkps.pop(t)
                tt(vb(QF, T)[:, cs], psq.t[:], vf(EX[0], 512), ALU.mult, [psq, EX[0]], [QF])
                tt(vb(QB, T)[:, cs], psq.t[:], vf(EX[2], 512), ALU.mult, [psq, EX[2]], [QB])
                if gla:
                    tt(vb(KF, T)[:, cs], psk.t[:], vf(EX[1], 512), ALU.mult, [psk, EX[1]], [KF])
                    tt(vb(KB, T)[:, cs], psk.t[:], vf(EX[3], 512), ALU.mult, [psk, EX[3]], [KB])
                else:
                    tt(vb(KF, T)[:, cs], vb(KF, T)[:, cs], vf(EX[1], 512), ALU.mult, [KF, EX[1]], [KF], eng="pool")
                    tt(vb(KB, T)[:, cs], vb(KB, T)[:, cs], vf(EX[3], 512), ALU.mult, [KB, EX[3]], [KB], eng="pool")

            qk_mm(0)
            for t in range(NT):
                factors(t)
                qk(t)
                if t > 0:
                    for s4 in range(4):
                        push(0, (t - 1) * 4 + s4)
            for s4 in range(4):
                push(0, (NT - 1) * 4 + s4)
            flush(0)
            for t in reversed(range(NT)):
                for s4 in reversed(range(4)):
                    push(1, t * 4 + s4)
            flush(1)

            gname = "glag" if gla else "hgg"
            cntc = {"po": 0, "pa": 0, "am": 0}

            def at_mm(rows, sc):
                cc = slice(sc * 128, (sc + 1) * 128)
                pa = BK[4 + cntc["pa"] % 2]
                cntc["pa"] += 1
                mm(pa, pa.t[:, 0:128], vb(KF, T)[rows, cc], vb(QF, T)[rows, cc], [KF, QF], signal=False)
                mm(pa, pa.t[:, 128:256], vb(KB, T)[rows, cc], vb(QB, T)[rows, cc], [KB, QB])
                return pa

            def main_c(t, hh):
                rows = slice(hh * 64, (hh + 1) * 64) if gla else slice(0, 128)
                po = BK[2 + cntc["po"] % 2]
                cntc["po"] += 1
                pa_next = at_mm(rows, t * 4)
                for s4 in range(4):
                    sc = t * 4 + s4
                    pa = pa_next
                    if s4 < 3:
                        pa_next = at_mm(rows, sc + 1)
                    am = AM[cntc["am"] % 2]
                    cntc["am"] += 1
                    tt(vb(am, 256), pa.t[:, 0:256], MSK.t[:, mk0:mk0 + 256], ALU.mult, [pa, MSK], [am])
                    oc_ = slice(s4 * 128, (s4 + 1) * 128)
                    vl = Vv[:, sc, hh * 128:(hh + 1) * 128]
                    mm(po, po.t[:, oc_], vl, vb(am, 256)[:, 0:128], [V, am], start=True, stop=False, signal=False)
                    mm(po, po.t[:, oc_], vl, vb(am, 256)[:, 128:256], [V, am], start=False, stop=False, signal=False)
                    for r in range(nsub):
                        c = nsub * sc + r
                        o2 = slice(s4 * 128 + r * CH, s4 * 128 + (r + 1) * CH)
                        q2 = slice(sc * 128 + r * CH, sc * 128 + (r + 1) * CH)
                        mm(po, po.t[:, o2], Sv[0][rows, c, :], vb(QF, T)[rows, q2], [SV[0], QF], start=False, stop=False, signal=False)
                        lastmm = (r == nsub - 1)
                        mm(po, po.t[:, o2], Sv[1][rows, c, :], vb(QB, T)[rows, q2], [SV[1], QB], start=False, stop=lastmm,
                           signal=(lastmm and s4 == 3))
                return po

            def post_c(t, hh, po):
                cs = slice(t * 512, (t + 1) * 512)
                hd = u * hpu + hh
                pg = bank2()
                for kc in range(KC):
                    mm(pg, pg.t[:], wBv[:, kc, hh * 128:(hh + 1) * 128], H.t[:, kc, cs], [wB, HT[t]], start=(kc == 0), stop=(kc == KC - 1),
                       signal=(kc == KC - 1))
                act(vb(SQ, 512), po.t[:], AF.Square, [po], [SQ])
                act(vf(SG, 512), pg.t[:], AF.Exp, [pg], [SG], scale=-1.0)
                act(vf(OTS, 512), po.t[:], AF.Copy, [po], [OTS])
                pn = bank2()
                mm(pn, pn.t[:], ONB.t[:], vb(SQ, 512), [ONB, SQ])
                act(vf(SG, 512), vf(SG, 512), AF.Ln, [SG, ONF], [SG], bias=ONF.t[:, 0:1])
                act(pn.t[:], pn.t[:], AF.Ln, [pn, EPSC], [pn], scale=1.0 / 128.0, bias=EPSC.t[:, 0:1])
                act(vf(SG, 512), vf(SG, 512), AF.Exp, [SG], [SG], scale=-1.0)
                act(pn.t[:], pn.t[:], AF.Exp, [pn], [pn], scale=-0.5)
                tt(vf(SG, 512), pg.t[:], vf(SG, 512), ALU.mult, [pg, SG], [SG])
                stt(vf(OTS, 512), vf(OTS, 512), prm(gname, l * 4 + hd), pn.t[:], ALU.mult, ALU.mult, [OTS, P_, pn], [OTS])
                tt(vb(OM, 4 * T).rearrange("p (h n) -> p h n", n=T)[:, hd, cs], vf(OTS, 512), vf(SG, 512), ALU.mult, [OTS, SG], [OM], eng="pool")

            prev = None
            for t in range(NT):
                for hh in range(hpu):
                    po = main_c(t, hh)
                    if prev is not None:
                        post_c(*prev)
                    prev = (t, hh, po)
            post_c(*prev)
            W.release(wAk)
            W.release(wBk)
        rot["banks"] = list(range(6))
        return

    b2 = {"i": 0}

    def bank2():
        b = BK[b2["i"] % 2]
        b2["i"] += 1
        assert b.last_w is None or b.last_w[0] != "pe" or b.readers, ("PSUM bank handed out before its previous result was read", b.name)
        return b

    def finish_mixer(l, gi, T, NT, m, norm_next=None):
        om = [r for r in C.live if r.name == "OM"][-1]
        OMv = vb(om, 4 * T).rearrange("p (h n) -> p h n", n=T)
        AR.reset(om.hi)
        CB = AR.get("CB", 8 * T * 2)
        CBv = vb(CB, 8 * T).rearrange("p (k n) -> p k n", n=T)
        TG = [AR.get("TG%d" % i, 512 * 4) for i in range(2)]
        nbufs = norm_alloc() if norm_next is not None else None
        br, brk = W.get(l, "br%d" % m)
        brv = br.t[:, 0:4096].rearrange("p (k n) -> p k n", n=1024)
        n = 0
        for half in range(2):
            wa, wak = W.get(l, "a%d_%d" % (m, half))
            wav = wa.t[:, 0:4096].rearrange("p (k n) -> p k n", n=512)
            for t in range(NT):
                cs = slice(t * 512, (t + 1) * 512)
                for oi in range(4):
                    oc = half * 4 + oi
                    pA = bank()
                    for kc in range(4):
                        mm(pA, pA.t[:], brv[:, kc, oc * 128:(oc + 1) * 128], OMv[:, kc, cs], [br, om], start=(kc == 0), stop=(kc == 3), signal=(kc == 3))
                    pB = bank()
                    for kc in range(KC):
                        mm(pB, pB.t[:], wav[:, kc, oi * 128:(oi + 1) * 128], H.t[:, kc, cs], [wa, HT[t]], start=(kc == 0), stop=(kc == KC - 1),
                           signal=(kc == KC - 1))
                    tg = TG[n % 2]
                    n += 1
                    act(vf(tg, 512), pB.t[:], AF.Tanh, [pB], [tg], scale=0.5)
                    stt(CBv[:, oc, cs], vf(tg, 512), 1.0, pA.t[:], ALU.add, ALU.mult, [tg, pA], [CB])
            W.release(wak)
        W.release(brk)
        wos = [W.get(l, "wo_%d" % half) for half in range(2)]
        for t in range(NT):
            cs = slice(t * 512, (t + 1) * 512)
            for half in range(2):
                wo = wos[half][0]
                wov = wo.t[:, 0:4096].rearrange("p (k n) -> p k n", n=512)
                for oi in range(4):
                    oc = half * 4 + oi
                    pO = bank()
                    for kc in range(KC):
                        mm(pO, pO.t[:], wov[:, kc, oi * 128:(oi + 1) * 128], CBv[:, kc, cs], [wo, CB], start=(kc == 0), stop=(kc == KC - 1),
                           signal=(kc == KC - 1))
                    stt(X.t[:, oc, cs], pO.t[:], DER.t[:, l, 2, oc, gi:gi + 1], X.t[:, oc, cs], ALU.mult, ALU.add, [pO, DER, XT[t]], [XT[t]])
            if norm_next is not None:
                norm_tile(norm_next[0], gi, norm_next[1], t, nbufs)
        for half in range(2):
            W.release(wos[half][1])

    def gmlp(l, gi, T, NT):
        AR.reset()
        OM = AR.get("OM", 4 * T * 2)
        OMv = vb(OM, 4 * T).rearrange("p (h n) -> p h n", n=T)
        U4s = [AR.get("U4_%d" % i, 4 * 512 * 2) for i in range(2)]
        U4vs = [vb(u_, 2048).rearrange("p (g n) -> p g n", n=512) for u_ in U4s]
        GV = [AR.get("GV%d" % i, 512 * 4) for i in range(2)]
        VN = [AR.get("VN%d" % i, 512 * 2) for i in range(2)]
        TM = [AR.get("TM%d" % i, 512 * 4) for i in range(2)]
        ST6 = AR.get("ST6", 8 * 4)
        MV2 = AR.get("MV2", 8 * 4)
        GG = AR.get("GG", 512 * 4)
        BS = AR.get("BS", 512 * 4)
        WST = AR.get("WST", 512 * 2)
        C.dma("sp", vf(GG, 512), d_gmg[l].partition_broadcast(128), writes=[GG])
        C.dma("sp", vf(BS, 512), d_gmb[l].partition_broadcast(128), writes=[BS])
        C.dma("pool", vb(WST, 512), d_wsT[l], writes=[WST])
        wmu, wmuk = W.get(l, "mu")
        wmv, wmvk = W.get(l, "mv")
        muv = wmu.t[:, 0:4096].rearrange("p (k n) -> p k n", n=512)
        mvv = wmv.t[:, 0:4096].rearrange("p (k n) -> p k n", n=512)
        n = 0
        ST6b = [ST6, AR.get("ST6b", 8 * 4)]
        MV2b = [MV2, AR.get("MV2b", 8 * 4)]

        def mvproj(t, s4):
            cc = slice((t * 4 + s4) * 128, (t * 4 + s4 + 1) * 128)
            psv = bank()
            for kc in range(KC):
                mm(psv, psv.t[:], H.t[:, kc, cc], mvv[:, kc, :], [HT[t], wmv], start=(kc == 0), stop=(kc == KC - 1), signal=(kc == KC - 1))
            return psv

        def muproj(t):
            cs = slice(t * 512, (t + 1) * 512)
            for g in range(4):
                ps = bank()
                for kc in range(KC):
                    mm(ps, ps.t[:], muv[:, kc, g * 128:(g + 1) * 128], H.t[:, kc, cs], [wmu, HT[t]], start=(kc == 0), stop=(kc == KC - 1),
                       signal=(kc == KC - 1))
                act(U4vs[t % 2][:, g, :], ps.t[:], AF.Gelu_apprx_tanh, [ps], [U4s[t % 2]])

        scs = [(t, s4) for t in range(NT) for s4 in range(4)]
        muproj(0)
        nxt = mvproj(*scs[0])
        for idx, (t, s4) in enumerate(scs):
            U4, U4v = U4s[t % 2], U4vs[t % 2]
            sc = t * 4 + s4
            cc = slice(sc * 128, (sc + 1) * 128)
            psv = nxt
            gv = GV[n % 2]
            vn = VN[n % 2]
            tm = TM[n % 2]
            st6 = ST6b[n % 2]
            mv2 = MV2b[n % 2]
            n += 1
            act(vf(gv, 512), psv.t[:], AF.Gelu_apprx_tanh, [psv], [gv])
            if idx + 1 < len(scs):
                if scs[idx + 1][1] == 0:
                    muproj(scs[idx + 1][0])
                nxt = mvproj(*scs[idx + 1])
            C.op("dve", lambda e, gv=gv, st6=st6: e.bn_stats(out=vf(st6, 6), in_=vf(gv, 512)), reads=[gv], writes=[st6])
            C.op("dve", lambda e, st6=st6, mv2=mv2: e.bn_aggr(out=vf(mv2, 2), in_=vf(st6, 6)), reads=[st6], writes=[mv2])
            ts(vf(mv2, 2)[:, 1:2], vf(mv2, 2)[:, 1:2], EPS, ALU.add, [mv2], [mv2], eng="pool")
            tt(vf(mv2, 2)[:, 1:2], vf(mv2, 2)[:, 1:2], EPSC.t[:, 3:4], ALU.pow, [mv2, EPSC], [mv2], eng="pool")
            ts(vf(gv, 512), vf(gv, 512), vf(mv2, 2)[:, 0:1], ALU.subtract, [gv, mv2], [gv], s2=vf(mv2, 2)[:, 1:2], op1=ALU.mult)
            tt(vb(vn, 512), vf(gv, 512), vf(GG, 512), ALU.mult, [gv, GG], [vn])
            pm = bank()
            for g in range(4):
                mm(pm, pm.t[:, g * 128:(g + 1) * 128], vb(vn, 512)[:, g * 128:(g + 1) * 128], vb(WST, 512)[:, g * 128:(g + 1) * 128], [vn, WST],
                   signal=(g == 3))
            tt(vf(tm, 512), pm.t[:], vf(BS, 512), ALU.add, [pm, BS], [tm])
            tt(OMv[:, :, cc], vf(tm, 512).rearrange("p (g n) -> p g n", n=128), U4v[:, :, s4 * 128:(s4 + 1) * 128], ALU.mult, [tm, U4], [OM])
        W.release(wmuk)
        W.release(wmvk)

    def ffn(l, gi, T, NT, norm_next=None):
        AR.reset()
        AF_ = [AR.get("AFF%d" % i, 4 * 512 * 2) for i in range(2)]
        RR = [AR.get("RR%d" % i, 512 * 4) for i in range(2)]
        nbufs = norm_alloc() if norm_next is not None else None
        n = 0
        rn = 0
        for f in range(8):
            w1, w1k = W.get(l, "w1_%d" % f)
            w2, w2k = W.get(l, "w2_%d" % f)
            w1v = w1.t[:, 0:4096].rearrange("p (k n) -> p k n", n=512)
            w2v = w2.t[:, 0:4096].rearrange("p (k n) -> p k n", n=1024)
            for t in range(NT):
                cs = slice(t * 512, (t + 1) * 512)
                af = AF_[n % 2]
                n += 1
                afv = vb(af, 2048).rearrange("p (k n) -> p k n", n=512)
                for fc in range(4):
                    ps = bank()
                    for kc in range(KC):
                        mm(ps, ps.t[:], w1v[:, kc, fc * 128:(fc + 1) * 128], H.t[:, kc, cs], [w1, HT[t]], start=(kc == 0), stop=(kc == KC - 1),
                           signal=(kc == KC - 1))
                    rr = RR[rn % 2]
                    rn += 1
                    act(vf(rr, 512), ps.t[:], AF.Relu, [ps], [rr])
                    act(afv[:, fc, :], vf(rr, 512), AF.Square, [rr], [af])
                for oc in range(KC):
                    ps = bank()
                    for kc in range(4):
                        mm(ps, ps.t[:], w2v[:, kc, oc * 128:(oc + 1) * 128], afv[:, kc, :], [w2, af], start=(kc == 0), stop=(kc == 3), signal=(kc == 3))
                    stt(X.t[:, oc, cs], ps.t[:], MOD.t[:, l, 40 + oc, gi:gi + 1], X.t[:, oc, cs], ALU.mult, ALU.add, [ps, MOD, XT[t]], [XT[t]])
                if f == 7 and norm_next is not None:
                    norm_tile(norm_next[0], gi, norm_next[1], t, nbufs)
            W.release(w1k)
            W.release(w2k)

    TAU = 29.0 * math.log(2.0)
    EPSC = C.sb("epsc", [128, 10], F32)
    for j, v in enumerate([EPS, -0.5 * math.log(64.0), -0.5 * math.log(128.0) - TAU, -0.5, 0.0, -TAU, 0.0, TAU, 2.0 * TAU]):
        C.op("dve", lambda e, j=j, v=v: e.memset(EPSC.t[:, j:j + 1], v), writes=[EPSC])

    for g in groups:
        run_group(g)
    C.finish()
    return nc, C


_CACHE = {}


def make_in_maps(inp, depth=DEPTH):
    f32 = lambda a: np.ascontiguousarray(np.asarray(a, dtype=np.float32))
    inp = {k: np.asarray(v) for k, v in inp.items()}
    wst = np.zeros((DEPTH, 128, WL), np.float32)
    for l in range(depth):
        s = layer_slices(inp, l)
        for n in SLICE_ORDER:
            wst[l, :, SLICE_OFF[n]:SLICE_OFF[n] + SLICE_W[n]] = s[n]
    lrw = f32(inp["gla_lr_w"].transpose(2, 0, 1, 3).reshape(16, DEPTH * 2 * 256))
    gmg = f32(inp["gm_norm_g"])
    gmb = f32(inp["gm_bs"].reshape(DEPTH, 512))
    wsT = f32(inp["gm_ws"].transpose(0, 3, 1, 2).reshape(DEPTH, 128, 512))
    maps = []
    for c in range(8):
        xs = f32(inp["x_sample"][c].T.reshape(KC, 128, T_S).transpose(1, 0, 2))
        xp = f32(inp["x_prompt"][2 * c:2 * c + 2].reshape(T_P, D).T.reshape(KC, 128, T_P).transpose(1, 0, 2))
        sg = f32(inp["state_gla"][c].reshape(DEPTH, 2, 2, 128, 128))
        sh = f32(inp["state_hgrn"][c])
        maps.append({"xs": xs, "xp": xp, "prm": pack_prm(inp, c), "wst": wst, "lrw": lrw, "gmg": gmg, "gmb": gmb, "wsT": wsT,
                     "sgla": sg, "shg": sh})
    return maps


def assemble(results):
    yp = np.zeros((16, 256, D), np.float32)
    ys = np.zeros((8, T_S, D), np.float32)
    ng = np.zeros((16, DEPTH, 2, 4, 64, 128), np.float32)
    nh = np.zeros((16, DEPTH, 2, 4, 128, 128), np.float32)
    for c, r in enumerate(results):
        ys[c] = r["ys"].transpose(1, 0, 2).reshape(D, T_S).T
        yp[2 * c:2 * c + 2] = r["yp"].transpose(1, 0, 2).reshape(D, T_P).T.reshape(2, 256, D)
        ng[2 * c:2 * c + 2] = r["ngla"].reshape(2, DEPTH, 2, 4, 64, 128)
        nh[2 * c:2 * c + 2] = r["nhg"]
    return yp, ys, ng, nh


def kernel(**inputs):
    if "nc" not in _CACHE:
        _CACHE["nc"] = build_nc()[0]
    nc = _CACHE["nc"]
    maps = make_in_maps(inputs)
    res = run_bass_kernel_spmd(nc, maps, core_ids=list(range(8)))
    return assemble(res.results)
```

```python
import math
import numpy as np
import concourse.bass as bass
import concourse.mybir as mybir
from concourse.bass_utils import run_bass_kernel_spmd

F32 = mybir.dt.float32
BF16 = mybir.dt.bfloat16
I32 = mybir.dt.int32
AF = mybir.ActivationFunctionType
ALU = mybir.AluOpType

D = 1024
KC = 8
DEPTH = 4
T_S = 2048
T_P = 512
EPS = 1e-6
NW = 3
A0_PAIR = 2
SLOT_E = 4096
ARENA_W = 23808

O_GQ, O_GK, O_GV, O_GO, O_GZF, O_GZB = 0, 256, 512, 1024, 1536, 1552
O_HQ, O_HFF, O_HFB, O_HI, O_HO = 1568, 2080, 2592, 3104, 3616
O_MU, O_MV, O_AGLA, O_AHG, O_AGM = 4128, 4640, 5152, 6176, 7200

PRM = {}
_o = 0
for _n, _w in [("g1", DEPTH * 8), ("g2", DEPTH * 8), ("adab", DEPTH * 48), ("lrb", DEPTH * 4), ("glag", DEPTH * 4),
               ("hgg", DEPTH * 4), ("gfin", 8), ("cc", 16), ("lg", 32)]:
    PRM[_n] = (_o, _w)
    _o += _w
NPRM = _o


def _fm(v):
    return np.ascontiguousarray(v.reshape(-1, 128).T)


def pack_prm(inp, core):
    a = np.zeros((128, NPRM), np.float32)

    def put(name, arr):
        o, w = PRM[name]
        assert arr.shape == (128, w), (name, arr.shape, w)
        a[:, o:o + w] = arr

    put("g1", np.concatenate([_fm(inp["norm_mix_g"][l]) for l in range(DEPTH)], 1))
    put("g2", np.concatenate([_fm(inp["norm_ffn_g"][l]) for l in range(DEPTH)], 1))
    put("adab", np.concatenate([_fm(inp["ada_b"][l]) for l in range(DEPTH)], 1))
    put("lrb", np.concatenate([_fm(inp["gla_lr_b"][l, d]) for l in range(DEPTH) for d in range(2)], 1))
    put("glag", np.concatenate([_fm(inp["gla_norm_g"][l]) for l in range(DEPTH)], 1))
    put("hgg", np.concatenate([_fm(inp["hg_norm_g"][l]) for l in range(DEPTH)], 1))
    put("gfin", _fm(inp["final_norm_g"]))
    cc = np.stack([_fm(inp["c"][core]), _fm(inp["c_ctx"])], -1).reshape(128, 16)
    put("cc", cc)
    lg = inp["hg_lb_logits"].reshape(2, DEPTH, 4, 128).transpose(3, 0, 2, 1).reshape(128, 32)
    put("lg", lg)
    return a


def _pk(M):
    kc = M.shape[0] // 128
    return M.reshape(kc, 128, M.shape[1]).transpose(1, 0, 2).reshape(128, kc * M.shape[1])


def layer_slices(inp, l):
    w = inp["w_in"][l]
    s = {}
    for i in range(12):
        s["ada%d" % i] = _pk(inp["ada_w"][l][:, i * 512:(i + 1) * 512])
    s["gz"] = _pk(w[:, O_GZF:O_GZF + 32])
    for u in range(2):
        s["gA%d" % u] = _pk(np.concatenate([w[:, O_GQ + u * 128:O_GQ + (u + 1) * 128], w[:, O_GK + u * 128:O_GK + (u + 1) * 128],
                                            w[:, O_GV + u * 256:O_GV + (u + 1) * 256]], 1))
        s["gB%d" % u] = _pk(w[:, O_GO + u * 256:O_GO + (u + 1) * 256])
    for u in range(4):
        sl = lambda o: w[:, o + u * 128:o + (u + 1) * 128]
        s["hA%d" % u] = _pk(np.concatenate([sl(O_HQ), sl(O_HFF), sl(O_HFB), sl(O_HI)], 1))
        s["hB%d" % u] = _pk(sl(O_HO))
    for m, (wb, oa) in enumerate([("w_br_gla", O_AGLA), ("w_br_hg", O_AHG), ("w_br_gm", O_AGM)]):
        s["br%d" % m] = _pk(inp[wb][l])
        s["a%d_0" % m] = _pk(w[:, oa:oa + 512])
        s["a%d_1" % m] = _pk(w[:, oa + 512:oa + 1024])
    s["wo_0"] = _pk(inp["w_out"][l][:, 0:512])
    s["wo_1"] = _pk(inp["w_out"][l][:, 512:1024])
    s["mu"] = _pk(w[:, O_MU:O_MU + 512])
    s["mv"] = _pk(w[:, O_MV:O_MV + 512])
    for f in range(8):
        s["w1_%d" % f] = _pk(inp["w_ff1"][l][:, f * 512:(f + 1) * 512])
        s["w2_%d" % f] = _pk(inp["w_ff2"][l][f * 512:(f + 1) * 512, :])
    return s


SLICE_W = {"gz": 256}
for _i in range(12):
    SLICE_W["ada%d" % _i] = 4096
for _u in range(2):
    SLICE_W["gA%d" % _u] = 4096
    SLICE_W["gB%d" % _u] = 2048
for _u in range(4):
    SLICE_W["hA%d" % _u] = 4096
    SLICE_W["hB%d" % _u] = 1024
for _m in range(3):
    SLICE_W["br%d" % _m] = 4096
    SLICE_W["a%d_0" % _m] = 4096
    SLICE_W["a%d_1" % _m] = 4096
SLICE_W["wo_0"] = 4096
SLICE_W["wo_1"] = 4096
SLICE_W["mu"] = 4096
SLICE_W["mv"] = 4096
for _f in range(8):
    SLICE_W["w1_%d" % _f] = 4096
    SLICE_W["w2_%d" % _f] = 4096
SLICE_ORDER = list(SLICE_W.keys())
SLICE_OFF = {}
_o = 0
for _n in SLICE_ORDER:
    SLICE_OFF[_n] = _o
    _o += SLICE_W[_n]
WL = _o


class Res:
    __slots__ = ("name", "t", "last_w", "readers", "lo", "hi")

    def __init__(self, name, t, lo=0, hi=0):
        self.name = name
        self.t = t
        self.last_w = None
        self.readers = {}
        self.lo = lo
        self.hi = hi


class Ctx:
    def __init__(self, nc):
        self.nc = nc
        self.eng = {"pe": nc.tensor, "act": nc.scalar, "dve": nc.vector, "pool": nc.gpsimd, "sp": nc.sync}
        self.sems = {}
        self.cnt = {}
        self.seen = {e: {} for e in self.eng}
        for e in ("pe", "act", "dve", "pool"):
            self.sems[e] = nc.alloc_semaphore("s_" + e)
            self.cnt[e] = 0
        self.pending = {e: False for e in self.eng}
        self.n_wait = 0
        self.n_ops = 0
        self.live = []
        self.dpool = {"sp": [self.dsem("sp%d" % i) for i in range(8)], "pool": [self.dsem("pq%d" % i) for i in range(4)]}
        self.dpi = {"sp": 0, "pool": 0}

    def sb(self, name, shape, dt):
        return Res(name, self.nc.alloc_sbuf_tensor("sb_" + name, list(shape), dt))

    def ps(self, name, shape=(128, 512), dt=F32):
        return Res(name, self.nc.alloc_psum_tensor("ps_" + name, list(shape), dt))

    def dsem(self, name):
        self.sems[name] = self.nc.alloc_semaphore("d_" + name)
        self.cnt[name] = 0
        return name

    def carve(self, name, arena, lo, hi):
        r = Res(name, arena.t, lo, hi)
        keep = []
        for o in self.live:
            if o.lo < hi and lo < o.hi:
                for k, v in o.readers.items():
                    if r.readers.get(k, 0) < v:
                        r.readers[k] = v
                if o.last_w is not None:
                    k, v = o.last_w
                    if r.readers.get(k, 0) < v:
                        r.readers[k] = v
                if not (o.lo >= lo and o.hi <= hi):
                    keep.append(o)
            else:
                keep.append(o)
        keep.append(r)
        self.live = keep
        return r

    def _need(self, e, reads, writes):
        need = {}

        def add(tok, raw):
            if tok is None:
                return
            k, v = tok
            if k == e and e == "pe":
                return
            if need.get(k, 0) < v:
                need[k] = v

        for r in reads:
            add(r.last_w, True)
        for w in writes:
            add(w.last_w, False)
            for k, v in w.readers.items():
                add((k, v), False)
        return need

    def _emit_waits(self, e, need):
        seen = self.seen[e]
        for k, v in need.items():
            if seen.get(k, 0) < v:
                self.eng[e].wait_ge(self.sems[k], v)
                seen[k] = v
                self.n_wait += 1

    def op(self, e, fn, reads=(), writes=(), signal=True, guards=()):
        need = self._need(e, reads, list(writes) + list(guards))
        self._emit_waits(e, need)
        inst = fn(self.eng[e])
        self.n_ops += 1
        if signal:
            self.cnt[e] += 1
            inst.then_inc(self.sems[e], 1)
            tok = (e, self.cnt[e])
            self.pending[e] = False
        else:
            tok = (e, self.cnt[e] + 1)
            self.pending[e] = True
        for r in reads:
            if r.readers.get(e, 0) < tok[1]:
                r.readers[e] = tok[1]
        for w in writes:
            w.last_w = tok
            w.readers = {}
        return inst

    def dma(self, e, out, in_, reads=(), writes=(), sem=None, **kw):
        if sem is None:
            pool = self.dpool[e]
            sem = pool[self.dpi[e] % len(pool)]
            self.dpi[e] += 1
        need = self._need(e, reads, writes)
        if self.cnt[sem] > 0 and need.get(sem, 0) < self.cnt[sem]:
            need[sem] = self.cnt[sem]
        self._emit_waits(e, need)
        inst = self.eng[e].dma_start(out=out, in_=in_, **kw)
        self.cnt[sem] += 16
        inst.then_inc(self.sems[sem], 16)
        tok = (sem, self.cnt[sem])
        for r in reads:
            if r.readers.get(sem, 0) < tok[1]:
                r.readers[sem] = tok[1]
        for w in writes:
            w.last_w = tok
            w.readers = {}
        return inst

    def finish(self):
        assert not any(self.pending.values()), self.pending
        e = "sp"
        for k, v in self.cnt.items():
            if v > 0 and self.seen[e].get(k, 0) < v:
                self.eng[e].wait_ge(self.sems[k], v)
                self.seen[e][k] = v


def build_nc(depth=DEPTH, groups=("P", "S")):
    nc = bass.Bass("TRN2", target_bir_lowering=False, dynamic_dma_scratch_size=4096)
    C = Ctx(nc)
    dram = lambda n, sh, kind="ExternalInput": nc.dram_tensor(n, list(sh), F32, kind=kind).ap()
    d_xs = dram("xs", [128, KC, T_S])
    d_xp = dram("xp", [128, KC, T_P])
    d_prm = dram("prm", [128, NPRM])
    d_w = dram("wst", [DEPTH, 128, WL])
    d_lrw = dram("lrw", [16, DEPTH * 2 * 256])
    d_gmg = dram("gmg", [DEPTH, 512])
    d_gmb = dram("gmb", [DEPTH, 512])
    d_wsT = dram("wsT", [DEPTH, 128, 512])
    d_sg = dram("sgla", [DEPTH, 2, 2, 128, 128])
    d_sh = dram("shg", [DEPTH, 2, 4, 128, 128])
    d_ys = dram("ys", [128, KC, T_S], "ExternalOutput")
    d_yp = dram("yp", [128, KC, T_P], "ExternalOutput")
    d_ng = dram("ngla", [2, DEPTH, 2, 2, 128, 128], "ExternalOutput")
    d_nh = dram("nhg", [2, DEPTH, 2, 4, 128, 128], "ExternalOutput")

    X = C.sb("X", [128, KC, T_S], F32)
    H = C.sb("H", [128, KC, T_S], BF16)
    XT = [Res("X%d" % i, X.t) for i in range(T_S // 512)]
    HT = [Res("H%d" % i, H.t) for i in range(T_S // 512)]
    WR = [C.sb("wr%d" % i, [128, SLOT_E], BF16) for i in range(NW)]
    P_ = C.sb("prm", [128, NPRM], F32)
    MOD = C.sb("mod", [128, DEPTH, 48, 2], F32)
    DER = C.sb("der", [128, DEPTH, 3, 8, 2], F32)
    LBT = C.sb("lbt", [128, 4, 2, 4, 4], F32)
    IDB = C.sb("idb", [128, 128], BF16)
    ONB = C.sb("onb", [128, 128], BF16)
    ONF = C.sb("onf", [128, 2], F32)
    MSK = C.sb("msk", [128, 768], BF16)
    RM = C.sb("rm", [128, 2], F32)
    ARENA = C.sb("arena", [128, ARENA_W], F32)
    BK = [C.ps("bk%d" % i) for i in range(6)]
    BTS = [C.ps("bt%d" % i, (128, 1024), BF16) for i in range(2)]

    WSEM = [C.dsem("w%d" % i) for i in range(NW)]
    GZW = C.sb("gzw", [128, 256], BF16)

    def prm(name, j):
        o, w = PRM[name]
        return P_.t[:, o + j:o + j + 1]

    class Ar:
        def __init__(self):
            self.cur = 0
            self.limit = ARENA_W

        def reset(self, to=0):
            self.cur = to

        def get(self, name, nbytes):
            nw = (nbytes + 3) // 4
            nw = (nw + 7) // 8 * 8
            lo = self.cur
            self.cur += nw
            assert self.cur <= self.limit, (name, self.cur, self.limit)
            return C.carve(name, ARENA, lo, lo + nw)

    AR = Ar()

    def vf(r, n=None):
        n = (r.hi - r.lo) if n is None else n
        return r.t[:, r.lo:r.lo + n]

    def vb(r, n=None):
        a = r.t[:, r.lo:r.hi].bitcast(BF16)
        return a if n is None else a[:, 0:n]

    rot = {"i": 0, "banks": list(range(6))}

    def bank():
        b = BK[rot["banks"][rot["i"] % len(rot["banks"])]]
        rot["i"] += 1
        assert b.last_w is None or b.last_w[0] != "pe" or b.readers, ("PSUM bank handed out before its previous result was read", b.name)
        return b

    def mm(ps, out, lhsT, rhs, reads, start=True, stop=True, signal=True):
        C.op("pe", lambda e: e.matmul(out, lhsT=lhsT, rhs=rhs, start=start, stop=stop), reads=reads, writes=[ps], signal=signal)

    def act(out, in_, func, reads, writes, scale=None, bias=None):
        kw = {}
        if scale is not None:
            kw["scale"] = scale
        if bias is not None:
            kw["bias"] = bias
        C.op("act", lambda e: e.activation(out=out, in_=in_, func=func, **kw), reads=reads, writes=writes)

    def tt(out, in0, in1, op, reads, writes, eng="dve"):
        C.op(eng, lambda e: e.tensor_tensor(out=out, in0=in0, in1=in1, op=op), reads=reads, writes=writes)

    def ts(out, in0, s1, op0, reads, writes, s2=None, op1=None, eng="dve"):
        if op1 is None:
            C.op(eng, lambda e: e.tensor_scalar(out=out, in0=in0, scalar1=s1, scalar2=None, op0=op0), reads=reads, writes=writes)
        else:
            C.op(eng, lambda e: e.tensor_scalar(out=out, in0=in0, scalar1=s1, scalar2=s2, op0=op0, op1=op1), reads=reads, writes=writes)

    def stt(out, in0, scalar, in1, op0, op1, reads, writes):
        C.op("dve", lambda e: e.scalar_tensor_tensor(out=out, in0=in0, scalar=scalar, in1=in1, op0=op0, op1=op1), reads=reads, writes=writes)

    def recip(out, in_, reads, writes):
        C.op("dve", lambda e: e.reciprocal(out=out, in_=in_), reads=reads, writes=writes)

    NXS = 6
    XS_LO = ARENA_W - NXS * 2048

    class View:
        def __init__(self, ap):
            self.ap = ap

        def __getitem__(self, idx):
            return self.ap[idx]

    XSL = []
    for i in range(NXS):
        lo = XS_LO + i * 2048
        r_ = Res("xs%d" % i, View(ARENA.t[:, lo:lo + 2048].bitcast(BF16)), lo, lo + 2048)
        XSL.append(r_)
    C.live.extend(XSL)
    SLOTS = WR + XSL
    WSEMX = WSEM + [C.dsem("wx%d" % i) for i in range(NXS)]

    class WRing:
        def __init__(self):
            self.seq = []
            self.prev = []
            self.last = {}
            self.issued = 0
            self.cur = 0
            self.released = set()

        def plan(self, items, slots):
            for it in items:
                k = len(self.seq)
                sl = slots[k % len(slots)] if len(slots) == NW else slots[self._rr % len(slots)]
                self._rr += 1
                self.seq.append((it[0], it[1], sl))
                self.prev.append(self.last.get(sl))
                self.last[sl] = k

        _rr = 0

        def _pump(self):
            while self.issued < len(self.seq):
                k = self.issued
                p = self.prev[k]
                if p is not None and p not in self.released:
                    break
                l, name, sl = self.seq[k]
                n = SLICE_W[name]
                o = SLICE_OFF[name]
                b = min(n, 1024)
                slot = SLOTS[sl]
                C.dma("pool", slot.t[:, 0:n].rearrange("p (a b) -> p a b", b=b), d_w[l, :, o:o + n].rearrange("p (a b) -> p a b", b=b),
                      writes=[slot], sem=WSEMX[sl])
                self.issued += 1

        def get(self, l, name):
            k = self.cur
            assert self.seq[k][0:2] == (l, name), (self.seq[k], l, name)
            self._pump()
            assert self.issued > k, ("weight ring stuck", k, name)
            self.cur += 1
            return SLOTS[self.seq[k][2]], k

        def release(self, k):
            self.released.add(k)
            self._pump()

    W = WRing()
    mixer_order = []
    for u in range(2):
        mixer_order += ["gA%d" % u, "gB%d" % u]
    mixer_order += ["br0", "a0_0", "a0_1", "wo_0", "wo_1"]
    for u in range(4):
        mixer_order += ["hA%d" % u, "hB%d" % u]
    mixer_order += ["br1", "a1_0", "a1_1", "wo_0", "wo_1"]
    mixer_order += ["mu", "mv", "br2", "a2_0", "a2_1", "wo_0", "wo_1"]
    for f in range(8):
        mixer_order += ["w1_%d" % f, "w2_%d" % f]
    for l in range(depth):
        W.plan([(l, "ada%d" % i) for i in range(12)], list(range(NW)))
    for g in groups:
        for l in range(depth):
            W.plan([(l, n) for n in mixer_order], list(range(NW + NXS)) if g == "P" else list(range(NW)))

    C.dma("sp", P_.t[:], d_prm, writes=[P_])
    C.op("pool", lambda e: e.memset(ONF.t[:], 1.0), writes=[ONF])
    AR.reset()
    scr = AR.get("scr", 128 * 4)
    C.op("pool", lambda e: e.memset(vf(scr, 128), 0.0), writes=[scr])
    C.op("pool", lambda e: e.affine_select(out=vf(scr, 128), in_=vf(scr, 128), pattern=[[-1, 128]], compare_op=ALU.not_equal,
                                           fill=1.0, base=0, channel_multiplier=1), reads=[scr], writes=[scr])
    C.op("dve", lambda e: e.tensor_copy(out=IDB.t[:], in_=vf(scr, 128)), reads=[scr], writes=[IDB])
    C.op("dve", lambda e: e.memset(ONB.t[:], 1.0), writes=[ONB])
    mscr = AR.get("mscr", 256 * 4)
    mv_ = vf(mscr, 256)
    C.op("pool", lambda e: e.memset(mv_, 0.0), writes=[mscr])
    for d_ in range(2):
        for b_ in range(2):
            reg = ARENA.t[b_ * 64:(b_ + 1) * 64, mscr.lo + d_ * 128 + b_ * 64: mscr.lo + d_ * 128 + (b_ + 1) * 64]
            C.op("pool", lambda e, reg=reg: e.memset(reg, 1.0), reads=[mscr], writes=[mscr])
            pat = [[1, 64]] if d_ == 0 else [[-1, 64]]
            cm = -1 if d_ == 0 else 1
            C.op("pool", lambda e, reg=reg, pat=pat, cm=cm: e.affine_select(out=reg, in_=reg, pattern=pat, compare_op=ALU.is_ge, fill=0.0,
                                                                            base=0, channel_multiplier=cm), reads=[mscr], writes=[mscr])
    C.op("dve", lambda e: e.tensor_copy(out=MSK.t[:, 0:256], in_=mv_), reads=[mscr], writes=[MSK])
    ts(MSK.t[:, 256:512], mv_, float(2.0 ** 58), ALU.mult, [mscr], [MSK])
    C.op("pool", lambda e: e.memset(mv_, 0.0), reads=[mscr], writes=[mscr])
    for d_ in range(2):
        for b_ in range(4):
            reg = ARENA.t[b_ * 32:(b_ + 1) * 32, mscr.lo + d_ * 128 + b_ * 32: mscr.lo + d_ * 128 + (b_ + 1) * 32]
            if b_ == 3:
                continue
            C.op("pool", lambda e, reg=reg: e.memset(reg, 1.0), reads=[mscr], writes=[mscr])
            pat = [[1, 32]] if d_ == 0 else [[-1, 32]]
            cm = -1 if d_ == 0 else 1
            C.op("pool", lambda e, reg=reg, pat=pat, cm=cm: e.affine_select(out=reg, in_=reg, pattern=pat, compare_op=ALU.is_ge, fill=0.0,
                                                                            base=0, channel_multiplier=cm), reads=[mscr], writes=[mscr])
        reg = ARENA.t[64:128, mscr.lo + d_ * 128 + 96: mscr.lo + d_ * 128 + 128]
        C.op("pool", lambda e, reg=reg: e.memset(reg, 1.0), reads=[mscr], writes=[mscr])
        pat = [[1, 32]] if d_ == 0 else [[-1, 32]]
        cm = -1 if d_ == 0 else 1
        C.op("pool", lambda e, reg=reg, pat=pat, cm=cm: e.affine_select(out=reg, in_=reg, pattern=pat, compare_op=ALU.is_ge, fill=0.0,
                                                                        base=(32 if cm == -1 else -32), channel_multiplier=cm), reads=[mscr], writes=[mscr])
        C.op("pool", lambda e, reg=reg: e.affine_select(out=reg, in_=reg, pattern=[[0, 32]], compare_op=ALU.is_ge, fill=0.0,
                                                        base=-32, channel_multiplier=1), reads=[mscr], writes=[mscr])
    ts(MSK.t[:, 512:768], mv_, float(2.0 ** 58), ALU.mult, [mscr], [MSK])
    C.op("pool", lambda e: e.memset(RM.t[:], 1.0), writes=[RM])
    C.op("pool", lambda e: e.affine_select(out=RM.t[:], in_=RM.t[:], pattern=[[0, 2]], compare_op=ALU.is_ge, fill=0.0, base=-96,
                                           channel_multiplier=1), reads=[RM], writes=[RM])

    o_lg = PRM["lg"][0]
    LG = P_.t[:, o_lg:o_lg + 32].rearrange("p (d h l) -> p d h l", d=2, h=4)
    ex = AR.get("lbex", 32 * 4)
    EXv = vf(ex, 32).rearrange("p (d h l) -> p d h l", d=2, h=4)
    sm = AR.get("lbsm", 8 * 4)
    SMv = vf(sm, 8).rearrange("p (d h) -> p d h", d=2)
    act(EXv, LG, AF.Exp, [P_], [ex])
    tt(SMv, EXv[:, :, :, 0], EXv[:, :, :, 1], ALU.add, [ex], [sm])
    tt(SMv, SMv, EXv[:, :, :, 2], ALU.add, [ex, sm], [sm])
    tt(SMv, SMv, EXv[:, :, :, 3], ALU.add, [ex, sm], [sm])
    recip(SMv, SMv, [sm], [sm])
    tt(EXv, EXv, vf(sm, 8).rearrange("p (d h) -> p d h", d=2).unsqueeze(3).broadcast_to([128, 2, 4, 4]), ALU.mult, [ex, sm], [ex])
    C.op("dve", lambda e: e.memset(LBT.t[:, 0, :, :, 0], 0.0), writes=[LBT])
    C.op("dve", lambda e: e.tensor_copy(out=LBT.t[:, 0, :, :, 1], in_=EXv[:, :, :, 1]), reads=[ex], writes=[LBT])
    tt(LBT.t[:, 0, :, :, 2], LBT.t[:, 0, :, :, 1], EXv[:, :, :, 2], ALU.add, [ex, LBT], [LBT])
    tt(LBT.t[:, 0, :, :, 3], LBT.t[:, 0, :, :, 2], EXv[:, :, :, 3], ALU.add, [ex, LBT], [LBT])
    ts(LBT.t[:, 1], LBT.t[:, 0], -1.0, ALU.mult, [LBT], [LBT], s2=1.0, op1=ALU.add)
    ts(LBT.t[:, 2], LBT.t[:, 1], -1.0, ALU.mult, [LBT], [LBT])
    o_lb = PRM["lrb"][0]
    lrb_v = P_.t[:, o_lb:o_lb + DEPTH * 4].rearrange("p (l d u) -> p d u l", d=2, u=2)
    ts(LBT.t[:, 3, :, 0:2, :], lrb_v, -1.0, ALU.mult, [P_], [LBT])

    o_cc = PRM["cc"][0]
    CCv = P_.t[:, o_cc:o_cc + 16]
    sc1 = AR.get("sc1", 16 * 4)
    scb = AR.get("scb", 16 * 2)
    act(vf(sc1, 16), CCv, AF.Exp, [P_], [sc1], scale=-1.0)
    ts(vf(sc1, 16), vf(sc1, 16), 1.0, ALU.add, [sc1], [sc1])
    recip(vf(sc1, 16), vf(sc1, 16), [sc1], [sc1])
    tt(vb(scb, 16), vf(sc1, 16), CCv, ALU.mult, [sc1, P_], [scb])
    SCB = vb(scb, 16).rearrange("p (k t) -> p k t", t=2)
    o_ab = PRM["adab"][0]
    for l in range(depth):
        for s in range(12):
            slot, wk = W.get(l, "ada%d" % s)
            sv = slot.t[:, 0:4096].rearrange("p (k n) -> p k n", n=512)
            ps = bank()
            for oi in range(4):
                for kc in range(KC):
                    mm(ps, ps.t[:, oi * 2:(oi + 1) * 2], sv[:, kc, oi * 128:(oi + 1) * 128], SCB[:, kc, :], [slot, scb],
                       start=(kc == 0), stop=(kc == KC - 1), signal=(kc == KC - 1 and oi == 3))
            ab = P_.t[:, o_ab + l * 48 + s * 4:o_ab + l * 48 + s * 4 + 4].unsqueeze(2).broadcast_to([128, 4, 2])
            tt(MOD.t[:, l, s * 4:(s + 1) * 4, :], ps.t[:, 0:8].rearrange("p (o t) -> p o t", t=2), ab, ALU.add, [ps, P_], [MOD])
            W.release(wk)
        o_g1 = PRM["g1"][0]
        o_g2 = PRM["g2"][0]
        for j, (og, mo) in enumerate([(o_g1, 8), (o_g2, 32)]):
            ts(DER.t[:, l, j], MOD.t[:, l, mo:mo + 8, :], 1.0, ALU.add, [MOD], [DER])
            gv = P_.t[:, og + l * 8:og + l * 8 + 8].unsqueeze(2).broadcast_to([128, 8, 2])
            tt(DER.t[:, l, j], DER.t[:, l, j], gv, ALU.mult, [DER, P_], [DER])
        ts(DER.t[:, l, 2], MOD.t[:, l, 16:24, :], 0.5, ALU.mult, [MOD], [DER])

    def run_group(grp):
        gi = 0 if grp == "S" else 1
        T = T_S if grp == "S" else T_P
        NT = T // 512
        NSC = T // 128
        NCH = T // 64
        seqs = [(0, 2048)] if grp == "S" else [(0, 256), (256, 256)]
        d_x = d_xs if grp == "S" else d_xp
        d_y = d_ys if grp == "S" else d_yp
        tl = lambda t: slice(t * 512, (t + 1) * 512)
        AR.limit = XS_LO if grp == "P" else ARENA_W

        for kc in range(KC):
            C.dma("sp", X.t[:, kc, 0:T], d_x[:, kc, :], writes=XT[0:NT])

        if grp == "S":
            posemb()

        for l in range(depth):
            if l == 0:
                norm_mod(l, gi, T, NT, 0)
            scan_mixer(l, gi, grp, T, NT, NSC, NCH, seqs, "gla")
            finish_mixer(l, gi, T, NT, 0)
            scan_mixer(l, gi, grp, T, NT, NSC, NCH, seqs, "hg")
            finish_mixer(l, gi, T, NT, 1)
            gmlp(l, gi, T, NT)
            finish_mixer(l, gi, T, NT, 2, norm_next=(l, 1))
            ffn(l, gi, T, NT, norm_next=((l + 1, 0) if l + 1 < depth else None))

        AR.reset()
        SQ8 = AR.get("sq8", 8 * 512 * 2)
        YT = [AR.get("yt%d" % i, 8 * 512 * 4) for i in range(2)]
        o_gf = PRM["gfin"][0]
        for t in range(NT):
            rs = rms_rstd(XT[t], lambda kc: X.t[:, kc, tl(t)], SQ8, 1.0 / D)
            y = YT[t % 2]
            yv = vf(y, 4096).rearrange("p (k n) -> p k n", n=512)
            for kc in range(KC):
                stt(yv[:, kc, :], X.t[:, kc, tl(t)], P_.t[:, o_gf + kc:o_gf + kc + 1], rs.t[:], ALU.mult, ALU.mult, [XT[t], P_, rs], [y])
            C.dma("sp", d_y[:, :, tl(t)], yv, reads=[y])

    def rms_rstd(src_res, src_ap, SQ8, inv_n):
        sq = vb(SQ8, 4096).rearrange("p (k n) -> p k n", n=512)
        for kc in range(KC):
            act(sq[:, kc, :], src_ap(kc), AF.Square, [src_res], [SQ8])
        ps = bank()
        for kc in range(KC):
            mm(ps, ps.t[:], ONB.t[:], sq[:, kc, :], [ONB, SQ8], start=(kc == 0), stop=(kc == KC - 1), signal=(kc == KC - 1))
        act(ps.t[:], ps.t[:], AF.Ln, [ps, EPSC], [ps], scale=inv_n, bias=EPSC.t[:, 0:1])
        act(ps.t[:], ps.t[:], AF.Exp, [ps], [ps], scale=-0.5)
        return ps

    nrm_cnt = {"n": 0}

    def norm_alloc():
        SQ8 = AR.get("sq8", 8 * 512 * 2)
        TT = [AR.get("tt%d" % i, 512 * 4) for i in range(3)]
        return SQ8, TT

    def norm_tile(l, gi, which, t, bufs):
        SQ8, TT = bufs
        sh0 = 0 if which == 0 else 24
        cs = slice(t * 512, (t + 1) * 512)
        rs = rms_rstd(XT[t], lambda kc: X.t[:, kc, cs], SQ8, 1.0 / D)
        for kc in range(KC):
            tb = TT[nrm_cnt["n"] % 3]
            nrm_cnt["n"] += 1
            stt(vf(tb, 512), X.t[:, kc, cs], DER.t[:, l, which, kc, gi:gi + 1], rs.t[:], ALU.mult, ALU.mult, [XT[t], DER, rs], [tb])
            act(H.t[:, kc, cs], vf(tb, 512), AF.Identity, [tb, MOD], [HT[t]], bias=MOD.t[:, l, sh0 + kc, gi:gi + 1])

    def norm_mod(l, gi, T, NT, which):
        AR.reset()
        bufs = norm_alloc()
        for t in range(NT):
            norm_tile(l, gi, which, t, bufs)

    def posemb():
        AR.reset()
        RI = AR.get("pe_r", 32 * 4)
        CI = AR.get("pe_c", 64 * 4)
        AG = AR.get("pe_a", 64 * 4)
        NI = AR.get("pe_n", 64 * 4)
        NF = AR.get("pe_f", 64 * 4)
        OMG = AR.get("pe_w", 2 * 4)
        C.op("pool", lambda e: e.iota(vf(RI, 32), pattern=[[1, 32]], base=0, channel_multiplier=0,
                                      allow_small_or_imprecise_dtypes=True), writes=[RI])
        C.op("pool", lambda e: e.iota(vf(CI, 64), pattern=[[1, 64]], base=0, channel_multiplier=0,
                                      allow_small_or_imprecise_dtypes=True), writes=[CI])
        C.op("pool", lambda e: e.iota(vf(OMG, 2), pattern=[[128, 2]], base=0, channel_multiplier=1,
                                      allow_small_or_imprecise_dtypes=True), writes=[OMG])
        act(vf(OMG, 2), vf(OMG, 2), AF.Exp, [OMG], [OMG], scale=-math.log(10000.0) / 256.0)
        TWO_PI = 2.0 * math.pi
        for kc in range(KC):
            L = 32 if kc < 4 else 64
            src = RI if kc < 4 else CI
            ph = 0.0 if (kc // 2) % 2 == 0 else math.pi / 2
            ts(vf(AG, L), vf(src, L), vf(OMG, 2)[:, kc % 2:kc % 2 + 1], ALU.mult, [src, OMG], [AG], s2=ph, op1=ALU.add)
            ni = ARENA.t[:, NI.lo:NI.lo + L].bitcast(I32)
            ts(ni, vf(AG, L), 1.0 / TWO_PI, ALU.mult, [AG], [NI])
            C.op("dve", lambda e, ni=ni, L=L: e.tensor_copy(out=vf(NF, L), in_=ni), reads=[NI], writes=[NF])
            stt(vf(AG, L), vf(NF, L), -TWO_PI, vf(AG, L), ALU.mult, ALU.add, [NF, AG], [AG])
            ts(vf(NF, L), vf(AG, L), math.pi, ALU.is_gt, [AG], [NF])
            stt(vf(AG, L), vf(NF, L), -TWO_PI, vf(AG, L), ALU.mult, ALU.add, [NF, AG], [AG])
            ts(vf(NF, L), vf(AG, L), -math.pi, ALU.is_lt, [AG], [NF])
            stt(vf(AG, L), vf(NF, L), TWO_PI, vf(AG, L), ALU.mult, ALU.add, [NF, AG], [AG])
            act(vf(NF, L), vf(AG, L), AF.Sin, [AG], [NF])
            xv = X.t[:, kc, :].rearrange("p (r c) -> p r c", c=64)
            if kc < 4:
                tb = vf(NF, 32).unsqueeze(2).broadcast_to([128, 32, 64])
            else:
                tb = vf(NF, 64).unsqueeze(1).broadcast_to([128, 32, 64])
            tt(xv, xv, tb, ALU.add, XT + [NF], XT)

    def scan_mixer(l, gi, grp, T, NT, NSC, NCH_unused, seqs, mix):
        gla = mix == "gla"
        es = (-1.0 / 16.0) if gla else 1.0
        nun = 2 if gla else 4
        hpu = 2 if gla else 1
        dk = 64 if gla else 128
        CH = 64 if (gla or l > 0) else 32
        NCH = T // CH
        nsub = 128 // CH
        HC = CH // 2
        mk0 = 0 if gla else (256 if CH == 64 else 512)
        AR.reset()
        OM = AR.get("OM", 4 * T * 2)
        base = AR.cur
        rot["banks"] = [0, 1, 2, 3]
        if gla:
            gzs = GZW
            og = SLICE_OFF["gz"]
            C.dma("pool", GZW.t[:, 0:256], d_w[l, :, og:og + 256], writes=[GZW])
        for u in range(nun):
            AR.reset(base)
            RW = max(NCH * 64 + 8, T + 8)
            QO = RW - (T + 8)
            QS = [AR.get("QS%d" % d, RW * 4) for d in range(2)]
            QK = [AR.get(n, T * 2) for n in ("QF", "QB", "KF", "KB")]
            QF, QB, KF, KB = QK
            V = AR.get("V", NSC * hpu * 128 * 2)
            EX = [AR.get("EX%d" % i, 512 * 4) for i in range(4)]
            UR = [AR.get("UR%d" % i, 128 * 4) for i in range(4)]
            KT = [AR.get("KT%d" % i, 128 * 2) for i in range(2)]
            AM = [AR.get("AM%d" % i, 512 * 2) for i in range(2)]
            SQ = AR.get("SQ", 512 * 2)
            OTS = AR.get("OTS", 512 * 4)
            SG = AR.get("SG", 512 * 4)
            CS = AR.get("CS", 2 * 3 * NCH * 4)
            CE = AR.get("CE", 2 * 6 * NCH * 4)
            ST = [[AR.get("ST%d%d" % (d, i), 128 * 4) for i in range(3 if nsub == 2 else 2)] for d in range(2)]
            if gla:
                GZ = [AR.get("GZ%d" % d, 512 * 2) for d in range(4)]
                LRW = AR.get("LRW", 512 * 2)
            KZ = AR.get("KZ", 128 * 2)
            Qv = [ARENA.t[:, q.lo + QO:q.lo + QO + T + 1] for q in QS]
            Sv = [ARENA.t[:, q.lo:q.lo + NCH * 64].bitcast(BF16).rearrange("p (c v) -> p c v", v=128) for q in QS]
            SV = [Res("SV%d" % d, ARENA.t, QS[d].lo, QS[d].hi) for d in range(2)]
            C.live.extend(SV)
            QT = [[Res("QT%d_%d" % (d, t), ARENA.t, QS[d].lo, QS[d].hi) for t in range(NT)] for d in range(2)]
            for d in range(2):
                C.live.extend(QT[d])
                for r_ in QT[d] + [SV[d]]:
                    r_.readers = dict(QS[d].readers)
            CSv = vf(CS, 6 * NCH).rearrange("p (d k c) -> p d k c", d=2, k=3)
            CEv = vf(CE, 12 * NCH).rearrange("p (d k c) -> p d k c", d=2, k=6)
            Vv = vb(V, NSC * hpu * 128).rearrange("p (s n) -> p s n", n=hpu * 128)
            wA, wAk = W.get(l, ("gA%d" if gla else "hA%d") % u)
            wAv = wA.t[:, 0:4096].rearrange("p (k n) -> p k n", n=512)
            wB, wBk = W.get(l, ("gB%d" if gla else "hB%d") % u)
            wBv = wB.t[:, 0:KC * hpu * 128].rearrange("p (k n) -> p k n", n=hpu * 128)
            if gla:
                C.dma("pool", vb(LRW, 512)[0:16, :], d_lrw[:, l * 512:(l + 1) * 512], writes=[LRW])
            if grp == "S":
                for d in range(2):
                    src = d_sg[l, d, u] if gla else d_sh[l, d, u]
                    C.dma("sp", vf(ST[d][0], 128), src, writes=[ST[d][0]])
            def vproj(t):
                for s4 in range(4):
                    sc = t * 4 + s4
                    psv = bank()
                    vcols = slice(256, 512) if gla else slice(384, 512)
                    for kc in range(KC):
                        mm(psv, psv.t[:, 0:hpu * 128], H.t[:, kc, sc * 128:(sc + 1) * 128], wAv[:, kc, vcols], [HT[t], wA],
                           start=(kc == 0), stop=(kc == KC - 1), signal=(kc == KC - 1))
                    C.op("dve", lambda e, sc=sc, psv=psv: e.tensor_copy(out=Vv[:, sc, :], in_=psv.t[:, 0:hpu * 128]), reads=[psv], writes=[V])

            for d in range(2):
                C.op("dve", lambda e, d=d: e.memset(Qv[d][:, 0:1], 0.0), writes=[QT[d][0]], guards=[QS[d]])

            def tile_scan(t):
                for d in range(2):
                    lo = 1 + t * 512
                    C.op("dve", lambda e, d=d, lo=lo: e.tensor_tensor_scan(out=Qv[d][:, lo:lo + 512], data0=ONF.t[:, 0:1].broadcast_to([128, 512]),
                                                                          data1=Qv[d][:, lo:lo + 512], initial=Qv[d][:, lo - 1:lo],
                                                                          op0=ALU.mult, op1=ALU.add),
                         reads=[QT[d][t], QT[d][max(t - 1, 0)], ONF], writes=[QT[d][t]])

            for tp in range(0, NT, A0_PAIR):
                tiles = [t for t in range(tp, tp + A0_PAIR) if t < NT]
                chs = [(t, d) for t in tiles for d in range(2)]
                exo = lambda t, d: EX[(t % 2) * 2 + d]
                csl = lambda t: slice(t * 512, (t + 1) * 512)
                qsl = lambda t: slice(1 + t * 512, 1 + (t + 1) * 512)
                for t in tiles:
                    vproj(t)
                pz = {}
                if gla:
                    gzv = gzs.t[:, 0:256].rearrange("p (k n) -> p k n", n=32)
                    for (t, d) in chs:
                        ps = bank()
                        for kc in range(KC):
                            mm(ps, ps.t[0:16, :], gzv[:, kc, d * 16:(d + 1) * 16], H.t[:, kc, csl(t)], [gzs, HT[t]], start=(kc == 0), stop=(kc == KC - 1),
                               signal=(kc == KC - 1))
                        pz[(t, d)] = ps
                    gzb = lambda t, d: GZ[(t % 2) * 2 + d]
                    for (t, d) in chs:
                        act(vb(gzb(t, d), 512)[0:16, :], pz[(t, d)].t[0:16, :], AF.Copy, [pz[(t, d)]], [gzb(t, d)])
                    pz2 = {}
                    for (t, d) in chs:
                        ps2 = bank()
                        mm(ps2, ps2.t[:], vb(LRW, 512)[0:16, d * 256 + u * 128:d * 256 + (u + 1) * 128], vb(gzb(t, d), 512)[0:16, :], [LRW, gzb(t, d)])
                        pz2[(t, d)] = ps2
                    for (t, d) in chs:
                        e_ = exo(t, d)
                        act(vf(e_, 512), pz2[(t, d)].t[:], AF.Exp, [pz2[(t, d)], LBT], [e_], scale=-1.0, bias=LBT.t[:, 3, d, u, l:l + 1])
                    for (t, d) in chs:
                        e_ = exo(t, d)
                        act(Qv[d][:, qsl(t)], vf(e_, 512), AF.Ln, [e_, ONF], [QT[d][t]], bias=ONF.t[:, 0:1])
                else:
                    for (t, d) in chs:
                        ps = bank()
                        for kc in range(KC):
                            mm(ps, ps.t[:], wAv[:, kc, (1 + d) * 128:(2 + d) * 128], H.t[:, kc, csl(t)], [wA, HT[t]], start=(kc == 0), stop=(kc == KC - 1),
                               signal=(kc == KC - 1))
                        pz[(t, d)] = ps
                    for (t, d) in chs:
                        e_ = exo(t, d)
                        act(vf(e_, 512), pz[(t, d)].t[:], AF.Exp, [pz[(t, d)]], [e_], scale=-1.0)
                    for (t, d) in chs:
                        e_ = exo(t, d)
                        act(vf(e_, 512), vf(e_, 512), AF.Ln, [e_, ONF], [e_], bias=ONF.t[:, 0:1])
                    for (t, d) in chs:
                        e_ = exo(t, d)
                        act(vf(e_, 512), vf(e_, 512), AF.Exp, [e_], [e_], scale=-1.0)
                    for (t, d) in chs:
                        e_ = exo(t, d)
                        act(Qv[d][:, qsl(t)], vf(e_, 512), AF.Ln, [e_, LBT], [QT[d][t]],
                            scale=LBT.t[:, 1, d, u, l:l + 1], bias=LBT.t[:, 0, d, u, l:l + 1])
                    for (t, d) in chs:
                        e_ = exo(t, d)
                        kd = KF if d == 0 else KB
                        ts(vb(kd, T)[:, csl(t)], vf(e_, 512), LBT.t[:, 2, d, u, l:l + 1], ALU.mult, [e_, LBT], [kd],
                           s2=LBT.t[:, 1, d, u, l:l + 1], op1=ALU.add)
                for t in tiles:
                    tile_scan(t)
            for d in range(2):
                q0 = Qv[d][:, 0:T].rearrange("p (c i) -> p c i", i=CH)
                q1 = Qv[d][:, 1:T + 1].rearrange("p (c i) -> p c i", i=CH)
                tt(CSv[:, d, 0, :], q0[:, :, HC], q0[:, :, 0], ALU.subtract, QT[d], [CS])
                tt(CSv[:, d, 1, :], q1[:, :, CH - 1], q0[:, :, HC], ALU.subtract, QT[d], [CS])
                C.op("dve", lambda e, d=d: e.memset(CSv[:, d, 2, NCH - 1:NCH], 0.0), writes=[CS])
                if NCH > 1:
                    tt(CSv[:, d, 2, 0:NCH - 1], q0[:, 1:NCH, HC], q0[:, 0:NCH - 1, HC], ALU.subtract, QT[d], [CS])
            bc = (lambda j: EPSC.t[:, 4:5]) if gla else (lambda j: EPSC.t[:, j:j + 1])
            act(CEv[:, :, 0:2, :], CSv[:, :, 0:2, :], AF.Exp, [CS, EPSC], [CE], scale=es, bias=bc(7))
            act(CEv[:, :, 2:4, :], CSv[:, :, 0:2, :], AF.Exp, [CS, EPSC], [CE], scale=es, bias=bc(5))
            act(CEv[:, :, 4, :], CSv[:, :, 2, :], AF.Exp, [CS], [CE], scale=es)
            act(CEv[:, :, 5, :], CSv[:, :, 2, :], AF.Exp, [CS, EPSC], [CE], scale=es, bias=bc(8))

            zi = [0, 0]

            def chunk_info(d, c):
                t0 = c * CH
                for si, (s0, sl) in enumerate(seqs):
                    if s0 <= t0 < s0 + sl:
                        break
                first = (t0 == s0) if d == 0 else (t0 + CH == s0 + sl)
                last = (t0 + CH == s0 + sl) if d == 0 else (t0 == s0)
                if last:
                    al = CEv[:, d, (3 if d == 0 else 2), c:c + 1]
                    be = CEv[:, d, (1 if d == 0 else 0), c:c + 1]
                else:
                    cj = c if d == 0 else c - 1
                    al = CEv[:, d, 4, cj:cj + 1]
                    be = CEv[:, d, 5, cj:cj + 1]
                return si, first, last, al, be

            cnt = {"kt": 0, "ur": 0, "pu": 0}

            def stage1(d, sc):
                kd = KF if d == 0 else KB
                sub = list(range(nsub)) if d == 0 else list(reversed(range(nsub)))
                ktn = cnt["kt"]
                cnt["kt"] += 1
                ptres = BTS[ktn % 2]
                ptr = ptres.t[:, 0:128]
                C.op("pe", lambda e: e.transpose(out=ptr, in_=vb(kd, T)[:, sc * 128:(sc + 1) * 128], identity=IDB.t[:]),
                     reads=[kd, IDB], writes=[ptres])
                kt = KT[ktn % 2]
                C.op("dve", lambda e: e.tensor_copy(out=vb(kt, 128), in_=ptr), reads=[ptres], writes=[kt])
                if CH == 32:
                    ts(vb(KZ, 128)[64:128, :], ptres.t[64:128, 0:128], RM.t[64:128, 0:1], ALU.mult, [ptres, RM], [KZ])
                items = []
                for r in sub:
                    c = nsub * sc + r
                    pun = cnt["pu"]
                    cnt["pu"] += 1
                    pur = BK[4 + pun % 2]
                    pu = pur.t[:, 0:128]
                    rows = slice(r * CH, (r + 1) * CH)
                    if gla:
                        for hh in range(2):
                            mm(pur, pur.t[hh * 64:(hh + 1) * 64, 0:128],
                               vb(kt, 128)[rows, hh * 64:(hh + 1) * 64], Vv[rows, sc, hh * 128:(hh + 1) * 128], [kt, V], signal=(hh == 1))
                    elif CH == 32 and r == 3:
                        mm(pur, pu, vb(KZ, 128)[64:128, :], Vv[64:128, sc, :], [KZ, V])
                    else:
                        mm(pur, pu, vb(kt, 128)[rows, :], Vv[rows, sc, :], [kt, V])
                    info = chunk_info(d, c)
                    ur = UR[cnt["ur"] % len(UR)]
                    cnt["ur"] += 1
                    act(vf(ur, 128), pu, AF.Identity, [pur, CE], [ur], scale=info[4])
                    items.append((c, ur, info))
                return items

            def stage2(d, items):
                for c, ur, (si, first, last, al, be) in items:
                    zc = ST[d][zi[d]]
                    zn = ST[d][(zi[d] + 1) % len(ST[d])]
                    if first:
                        if grp == "P":
                            C.op("dve", lambda e, zc=zc: e.memset(vf(zc, 128), 0.0), writes=[zc])
                        else:
                            act(vf(zc, 128), vf(zc, 128), AF.Identity, [zc, CE], [zc], scale=CEv[:, d, (0 if d == 0 else 1), c:c + 1])
                    C.op("act", lambda e, zc=zc, c=c: e.activation(out=Sv[d][:, c, :], in_=vf(zc, 128), func=AF.Copy), reads=[zc], writes=[SV[d]],
                         guards=QT[d])
                    stt(vf(zn, 128), vf(zc, 128), al, vf(ur, 128), ALU.mult, ALU.add, [zc, CE, ur], [zn])
                    zi[d] = (zi[d] + 1) % len(ST[d])
                    if last and grp == "P":
                        dst = d_ng[si, l, d, u] if gla else d_nh[si, l, d, u]
                        C.dma("sp", dst, vf(zn, 128), reads=[zn])

            pend = {0: None, 1: None}

            def push(d, sc):
                items = stage1(d, sc)
                if 2 * nsub > len(UR):
                    stage2(d, items)
                    return
                if pend[d] is not None:
                    stage2(d, pend[d])
                pend[d] = items

            def flush(d):
                if pend[d] is not None:
                    stage2(d, pend[d])
                    pend[d] = None

            def factors(t):
                for d in range(2):
                    xa = EX[2 * d]
                    off = 1 if d == 0 else 0
                    qq = Qv[d][:, off + t * 512:off + (t + 1) * 512].rearrange("p (c i) -> p c i", i=CH)
                    ref = Qv[d][:, t * 512:(t + 1) * 512].rearrange("p (c i) -> p c i", i=CH)[:, :, HC:HC + 1].broadcast_to([128, 512 // CH, CH])
                    tt(vf(xa, 512).rearrange("p (c i) -> p c i", i=CH), qq, ref, ALU.subtract, [QT[d][t], QT[d][max(t - 1, 0)]], [xa])
                    if not gla and l == 0:
                        ts(vf(xa, 512), vf(xa, 512), 62.0, ALU.min, [xa], [xa], s2=-62.0, op1=ALU.max)
                for d in range(2):
                    xa, xb = EX[2 * d], EX[2 * d + 1]
                    sq_ = es if d == 0 else -es
                    act(vf(xb, 512), vf(xa, 512), AF.Exp, [xa, EPSC], [xb], scale=-sq_, bias=EPSC.t[:, (4 if gla else 5):(5 if gla else 6)])
                for d in range(2):
                    xa = EX[2 * d]
                    sq_ = es if d == 0 else -es
                    act(vf(xa, 512), vf(xa, 512), AF.Exp, [xa, EPSC], [xa], scale=sq_, bias=EPSC.t[:, (1 if gla else 2):(2 if gla else 3)])

            qkps = {}

            def qk_mm(t):
                cs = slice(t * 512, (t + 1) * 512)
                psq = bank()
                for kc in range(KC):
                    mm(psq, psq.t[:], wAv[:, kc, 0:128], H.t[:, kc, cs], [wA, HT[t]], start=(kc == 0), stop=(kc == KC - 1), signal=(kc == KC - 1))
                psk = None
                if gla:
                    psk = bank()
                    for kc in range(KC):
                        mm(psk, psk.t[:], wAv[:, kc, 128:256], H.t[:, kc, cs], [wA, HT[t]], start=(kc == 0), stop=(kc == KC - 1), signal=(kc == KC - 1))
                qkps[t] = (psq, psk)

            def qk(t):
                cs = slice(t * 512, (t + 1) * 512)
                if t + 1 < NT:
                    qk_mm(t + 1)
                psq, psk = qkps.pop(t)
                tt(vb(QF, T)[:, cs], psq.t[:], vf(EX[0], 512), ALU.mult, [psq, EX[0]], [QF])
                tt(vb(QB, T)[:, cs], psq.t[:], vf(EX[2], 512), ALU.mult, [psq, EX[2]], [QB])
                if gla:
                    tt(vb(KF, T)[:, cs], psk.t[:], vf(EX[1], 512), ALU.mult, [psk, EX[1]], [KF])
                    tt(vb(KB, T)[:, cs], psk.t[:], vf(EX[3], 512), ALU.mult, [psk, EX[3]], [KB])
                else:
                    tt(vb(KF, T)[:, cs], vb(KF, T)[:, cs], vf(EX[1], 512), ALU.mult, [KF, EX[1]], [KF], eng="pool")
                    tt(vb(KB, T)[:, cs], vb(KB, T)[:, cs], vf(EX[3], 512), ALU.mult, [KB, EX[3]], [KB], eng="pool")

            qk_mm(0)
            for t in range(NT):
                factors(t)
                qk(t)
                if t > 0:
                    for s4 in range(4):
                        push(0, (t - 1) * 4 + s4)
            for s4 in range(4):
                push(0, (NT - 1) * 4 + s4)
            flush(0)
            for t in reversed(range(NT)):
                for s4 in reversed(range(4)):
                    push(1, t * 4 + s4)
            flush(1)

            gname = "glag" if gla else "hgg"
            cntc = {"po": 0, "pa": 0, "am": 0}

            def at_mm(rows, sc):
                cc = slice(sc * 128, (sc + 1) * 128)
                pa = BK[4 + cntc["pa"] % 2]
                cntc["pa"] += 1
                mm(pa, pa.t[:, 0:128], vb(KF, T)[rows, cc], vb(QF, T)[rows, cc], [KF, QF], signal=False)
                mm(pa, pa.t[:, 128:256], vb(KB, T)[rows, cc], vb(QB, T)[rows, cc], [KB, QB])
                return pa

            def main_c(t, hh):
                rows = slice(hh * 64, (hh + 1) * 64) if gla else slice(0, 128)
                po = BK[2 + cntc["po"] % 2]
                cntc["po"] += 1
                pa_next = at_mm(rows, t * 4)
                for s4 in range(4):
                    sc = t * 4 + s4
                    pa = pa_next
                    if s4 < 3:
                        pa_next = at_mm(rows, sc + 1)
                    am = AM[cntc["am"] % 2]
                    cntc["am"] += 1
                    tt(vb(am, 256), pa.t[:, 0:256], MSK.t[:, mk0:mk0 + 256], ALU.mult, [pa, MSK], [am])
                    oc_ = slice(s4 * 128, (s4 + 1) * 128)
                    vl = Vv[:, sc, hh * 128:(hh + 1) * 128]
                    mm(po, po.t[:, oc_], vl, vb(am, 256)[:, 0:128], [V, am], start=True, stop=False, signal=False)
                    mm(po, po.t[:, oc_], vl, vb(am, 256)[:, 128:256], [V, am], start=False, stop=False, signal=False)
                    for r in range(nsub):
                        c = nsub * sc + r
                        o2 = slice(s4 * 128 + r * CH, s4 * 128 + (r + 1) * CH)
                        q2 = slice(sc * 128 + r * CH, sc * 128 + (r + 1) * CH)
                        mm(po, po.t[:, o2], Sv[0][rows, c, :], vb(QF, T)[rows, q2], [SV[0], QF], start=False, stop=False, signal=False)
                        lastmm = (r == nsub - 1)
                        mm(po, po.t[:, o2], Sv[1][rows, c, :], vb(QB, T)[rows, q2], [SV[1], QB], start=False, stop=lastmm,
                           signal=(lastmm and s4 == 3))
                return po

            def post_c(t, hh, po):
                cs = slice(t * 512, (t + 1) * 512)
                hd = u * hpu + hh
                pg = bank2()
                for kc in range(KC):
                    mm(pg, pg.t[:], wBv[:, kc, hh * 128:(hh + 1) * 128], H.t[:, kc, cs], [wB, HT[t]], start=(kc == 0), stop=(kc == KC - 1),
                       signal=(kc == KC - 1))
                act(vb(SQ, 512), po.t[:], AF.Square, [po], [SQ])
                act(vf(SG, 512), pg.t[:], AF.Exp, [pg], [SG], scale=-1.0)
                act(vf(OTS, 512), po.t[:], AF.Copy, [po], [OTS])
                pn = bank2()
                mm(pn, pn.t[:], ONB.t[:], vb(SQ, 512), [ONB, SQ])
                act(vf(SG, 512), vf(SG, 512), AF.Ln, [SG, ONF], [SG], bias=ONF.t[:, 0:1])
                act(pn.t[:], pn.t[:], AF.Ln, [pn, EPSC], [pn], scale=1.0 / 128.0, bias=EPSC.t[:, 0:1])
                act(vf(SG, 512), vf(SG, 512), AF.Exp, [SG], [SG], scale=-1.0)
                act(pn.t[:], pn.t[:], AF.Exp, [pn], [pn], scale=-0.5)
                tt(vf(SG, 512), pg.t[:], vf(SG, 512), ALU.mult, [pg, SG], [SG])
                stt(vf(OTS, 512), vf(OTS, 512), prm(gname, l * 4 + hd), pn.t[:], ALU.mult, ALU.mult, [OTS, P_, pn], [OTS])
                tt(vb(OM, 4 * T).rearrange("p (h n) -> p h n", n=T)[:, hd, cs], vf(OTS, 512), vf(SG, 512), ALU.mult, [OTS, SG], [OM], eng="pool")

            prev = None
            for t in range(NT):
                for hh in range(hpu):
                    po = main_c(t, hh)
                    if prev is not None:
                        post_c(*prev)
                    prev = (t, hh, po)
            post_c(*prev)
            W.release(wAk)
            W.release(wBk)
        rot["banks"] = list(range(6))
        return

    b2 = {"i": 0}

    def bank2():
        b = BK[b2["i"] % 2]
        b2["i"] += 1
        assert b.last_w is None or b.last_w[0] != "pe" or b.readers, ("PSUM bank handed out before its previous result was read", b.name)
        return b

    def finish_mixer(l, gi, T, NT, m, norm_next=None):
        om = [r for r in C.live if r.name == "OM"][-1]
        OMv = vb(om, 4 * T).rearrange("p (h n) -> p h n", n=T)
        AR.reset(om.hi)
        CB = AR.get("CB", 8 * T * 2)
        CBv = vb(CB, 8 * T).rearrange("p (k n) -> p k n", n=T)
        TG = [AR.get("TG%d" % i, 512 * 4) for i in range(2)]
        nbufs = norm_alloc() if norm_next is not None else None
        br, brk = W.get(l, "br%d" % m)
        brv = br.t[:, 0:4096].rearrange("p (k n) -> p k n", n=1024)
        n = 0
        for half in range(2):
            wa, wak = W.get(l, "a%d_%d" % (m, half))
            wav = wa.t[:, 0:4096].rearrange("p (k n) -> p k n", n=512)
            for t in range(NT):
                cs = slice(t * 512, (t + 1) * 512)
                for oi in range(4):
                    oc = half * 4 + oi
                    pA = bank()
                    for kc in range(4):
                        mm(pA, pA.t[:], brv[:, kc, oc * 128:(oc + 1) * 128], OMv[:, kc, cs], [br, om], start=(kc == 0), stop=(kc == 3), signal=(kc == 3))
                    pB = bank()
                    for kc in range(KC):
                        mm(pB, pB.t[:], wav[:, kc, oi * 128:(oi + 1) * 128], H.t[:, kc, cs], [wa, HT[t]], start=(kc == 0), stop=(kc == KC - 1),
                           signal=(kc == KC - 1))
                    tg = TG[n % 2]
                    n += 1
                    act(vf(tg, 512), pB.t[:], AF.Tanh, [pB], [tg], scale=0.5)
                    stt(CBv[:, oc, cs], vf(tg, 512), 1.0, pA.t[:], ALU.add, ALU.mult, [tg, pA], [CB])
            W.release(wak)
        W.release(brk)
        wos = [W.get(l, "wo_%d" % half) for half in range(2)]
        for t in range(NT):
            cs = slice(t * 512, (t + 1) * 512)
            for half in range(2):
                wo = wos[half][0]
                wov = wo.t[:, 0:4096].rearrange("p (k n) -> p k n", n=512)
                for oi in range(4):
                    oc = half * 4 + oi
                    pO = bank()
                    for kc in range(KC):
                        mm(pO, pO.t[:], wov[:, kc, oi * 128:(oi + 1) * 128], CBv[:, kc, cs], [wo, CB], start=(kc == 0), stop=(kc == KC - 1),
                           signal=(kc == KC - 1))
                    stt(X.t[:, oc, cs], pO.t[:], DER.t[:, l, 2, oc, gi:gi + 1], X.t[:, oc, cs], ALU.mult, ALU.add, [pO, DER, XT[t]], [XT[t]])
            if norm_next is not None:
                norm_tile(norm_next[0], gi, norm_next[1], t, nbufs)
        for half in range(2):
            W.release(wos[half][1])

    def gmlp(l, gi, T, NT):
        AR.reset()
        OM = AR.get("OM", 4 * T * 2)
        OMv = vb(OM, 4 * T).rearrange("p (h n) -> p h n", n=T)
        U4s = [AR.get("U4_%d" % i, 4 * 512 * 2) for i in range(2)]
        U4vs = [vb(u_, 2048).rearrange("p (g n) -> p g n", n=512) for u_ in U4s]
        GV = [AR.get("GV%d" % i, 512 * 4) for i in range(2)]
        VN = [AR.get("VN%d" % i, 512 * 2) for i in range(2)]
        TM = [AR.get("TM%d" % i, 512 * 4) for i in range(2)]
        ST6 = AR.get("ST6", 8 * 4)
        MV2 = AR.get("MV2", 8 * 4)
        GG = AR.get("GG", 512 * 4)
        BS = AR.get("BS", 512 * 4)
        WST = AR.get("WST", 512 * 2)
        C.dma("sp", vf(GG, 512), d_gmg[l].partition_broadcast(128), writes=[GG])
        C.dma("sp", vf(BS, 512), d_gmb[l].partition_broadcast(128), writes=[BS])
        C.dma("pool", vb(WST, 512), d_wsT[l], writes=[WST])
        wmu, wmuk = W.get(l, "mu")
        wmv, wmvk = W.get(l, "mv")
        muv = wmu.t[:, 0:4096].rearrange("p (k n) -> p k n", n=512)
        mvv = wmv.t[:, 0:4096].rearrange("p (k n) -> p k n", n=512)
        n = 0
        ST6b = [ST6, AR.get("ST6b", 8 * 4)]
        MV2b = [MV2, AR.get("MV2b", 8 * 4)]

        def mvproj(t, s4):
            cc = slice((t * 4 + s4) * 128, (t * 4 + s4 + 1) * 128)
            psv = bank()
            for kc in range(KC):
                mm(psv, psv.t[:], H.t[:, kc, cc], mvv[:, kc, :], [HT[t], wmv], start=(kc == 0), stop=(kc == KC - 1), signal=(kc == KC - 1))
            return psv

        def muproj(t):
            cs = slice(t * 512, (t + 1) * 512)
            for g in range(4):
                ps = bank()
                for kc in range(KC):
                    mm(ps, ps.t[:], muv[:, kc, g * 128:(g + 1) * 128], H.t[:, kc, cs], [wmu, HT[t]], start=(kc == 0), stop=(kc == KC - 1),
                       signal=(kc == KC - 1))
                act(U4vs[t % 2][:, g, :], ps.t[:], AF.Gelu_apprx_tanh, [ps], [U4s[t % 2]])

        scs = [(t, s4) for t in range(NT) for s4 in range(4)]
        muproj(0)
        nxt = mvproj(*scs[0])
        for idx, (t, s4) in enumerate(scs):
            U4, U4v = U4s[t % 2], U4vs[t % 2]
            sc = t * 4 + s4
            cc = slice(sc * 128, (sc + 1) * 128)
            psv = nxt
            gv = GV[n % 2]
            vn = VN[n % 2]
            tm = TM[n % 2]
            st6 = ST6b[n % 2]
            mv2 = MV2b[n % 2]
            n += 1
            act(vf(gv, 512), psv.t[:], AF.Gelu_apprx_tanh, [psv], [gv])
            if idx + 1 < len(scs):
                if scs[idx + 1][1] == 0:
                    muproj(scs[idx + 1][0])
                nxt = mvproj(*scs[idx + 1])
            C.op("dve", lambda e, gv=gv, st6=st6: e.bn_stats(out=vf(st6, 6), in_=vf(gv, 512)), reads=[gv], writes=[st6])
            C.op("dve", lambda e, st6=st6, mv2=mv2: e.bn_aggr(out=vf(mv2, 2), in_=vf(st6, 6)), reads=[st6], writes=[mv2])
            ts(vf(mv2, 2)[:, 1:2], vf(mv2, 2)[:, 1:2], EPS, ALU.add, [mv2], [mv2], eng="pool")
            tt(vf(mv2, 2)[:, 1:2], vf(mv2, 2)[:, 1:2], EPSC.t[:, 3:4], ALU.pow, [mv2, EPSC], [mv2], eng="pool")
            ts(vf(gv, 512), vf(gv, 512), vf(mv2, 2)[:, 0:1], ALU.subtract, [gv, mv2], [gv], s2=vf(mv2, 2)[:, 1:2], op1=ALU.mult)
            tt(vb(vn, 512), vf(gv, 512), vf(GG, 512), ALU.mult, [gv, GG], [vn])
            pm = bank()
            for g in range(4):
                mm(pm, pm.t[:, g * 128:(g + 1) * 128], vb(vn, 512)[:, g * 128:(g + 1) * 128], vb(WST, 512)[:, g * 128:(g + 1) * 128], [vn, WST],
                   signal=(g == 3))
            tt(vf(tm, 512), pm.t[:], vf(BS, 512), ALU.add, [pm, BS], [tm])
            tt(OMv[:, :, cc], vf(tm, 512).rearrange("p (g n) -> p g n", n=128), U4v[:, :, s4 * 128:(s4 + 1) * 128], ALU.mult, [tm, U4], [OM])
        W.release(wmuk)
        W.release(wmvk)

    def ffn(l, gi, T, NT, norm_next=None):
        AR.reset()
        AF_ = [AR.get("AFF%d" % i, 4 * 512 * 2) for i in range(2)]
        RR = [AR.get("RR%d" % i, 512 * 4) for i in range(2)]
        nbufs = norm_alloc() if norm_next is not None else None
        n = 0
        rn = 0
        for f in range(8):
            w1, w1k = W.get(l, "w1_%d" % f)
            w2, w2k = W.get(l, "w2_%d" % f)
            w1v = w1.t[:, 0:4096].rearrange("p (k n) -> p k n", n=512)
            w2v = w2.t[:, 0:4096].rearrange("p (k n) -> p k n", n=1024)
            for t in range(NT):
                cs = slice(t * 512, (t + 1) * 512)
                af = AF_[n % 2]
                n += 1
                afv = vb(af, 2048).rearrange("p (k n) -> p k n", n=512)
                for fc in range(4):
                    ps = bank()
                    for kc in range(KC):
                        mm(ps, ps.t[:], w1v[:, kc, fc * 128:(fc + 1) * 128], H.t[:, kc, cs], [w1, HT[t]], start=(kc == 0), stop=(kc == KC - 1),
                           signal=(kc == KC - 1))
                    rr = RR[rn % 2]
                    rn += 1
                    act(vf(rr, 512), ps.t[:], AF.Relu, [ps], [rr])
                    act(afv[:, fc, :], vf(rr, 512), AF.Square, [rr], [af])
                for oc in range(KC):
                    ps = bank()
                    for kc in range(4):
                        mm(ps, ps.t[:], w2v[:, kc, oc * 128:(oc + 1) * 128], afv[:, kc, :], [w2, af], start=(kc == 0), stop=(kc == 3), signal=(kc == 3))
                    stt(X.t[:, oc, cs], ps.t[:], MOD.t[:, l, 40 + oc, gi:gi + 1], X.t[:, oc, cs], ALU.mult, ALU.add, [ps, MOD, XT[t]], [XT[t]])
                if f == 7 and norm_next is not None:
                    norm_tile(norm_next[0], gi, norm_next[1], t, nbufs)
            W.release(w1k)
            W.release(w2k)

    TAU = 29.0 * math.log(2.0)
    EPSC = C.sb("epsc", [128, 10], F32)
    for j, v in enumerate([EPS, -0.5 * math.log(64.0), -0.5 * math.log(128.0) - TAU, -0.5, 0.0, -TAU, 0.0, TAU, 2.0 * TAU]):
        C.op("dve", lambda e, j=j, v=v: e.memset(EPSC.t[:, j:j + 1], v), writes=[EPSC])

    for g in groups:
        run_group(g)
    C.finish()
    return nc, C


_CACHE = {}


def make_in_maps(inp, depth=DEPTH):
    f32 = lambda a: np.ascontiguousarray(np.asarray(a, dtype=np.float32))
    inp = {k: np.asarray(v) for k, v in inp.items()}
    wst = np.zeros((DEPTH, 128, WL), np.float32)
    for l in range(depth):
        s = layer_slices(inp, l)
        for n in SLICE_ORDER:
            wst[l, :, SLICE_OFF[n]:SLICE_OFF[n] + SLICE_W[n]] = s[n]
    lrw = f32(inp["gla_lr_w"].transpose(2, 0, 1, 3).reshape(16, DEPTH * 2 * 256))
    gmg = f32(inp["gm_norm_g"])
    gmb = f32(inp["gm_bs"].reshape(DEPTH, 512))
    wsT = f32(inp["gm_ws"].transpose(0, 3, 1, 2).reshape(DEPTH, 128, 512))
    maps = []
    for c in range(8):
        xs = f32(inp["x_sample"][c].T.reshape(KC, 128, T_S).transpose(1, 0, 2))
        xp = f32(inp["x_prompt"][2 * c:2 * c + 2].reshape(T_P, D).T.reshape(KC, 128, T_P).transpose(1, 0, 2))
        sg = f32(inp["state_gla"][c].reshape(DEPTH, 2, 2, 128, 128))
        sh = f32(inp["state_hgrn"][c])
        maps.append({"xs": xs, "xp": xp, "prm": pack_prm(inp, c), "wst": wst, "lrw": lrw, "gmg": gmg, "gmb": gmb, "wsT": wsT,
                     "sgla": sg, "shg": sh})
    return maps


def assemble(results):
    yp = np.zeros((16, 256, D), np.float32)
    ys = np.zeros((8, T_S, D), np.float32)
    ng = np.zeros((16, DEPTH, 2, 4, 64, 128), np.float32)
    nh = np.zeros((16, DEPTH, 2, 4, 128, 128), np.float32)
    for c, r in enumerate(results):
        ys[c] = r["ys"].transpose(1, 0, 2).reshape(D, T_S).T
        yp[2 * c:2 * c + 2] = r["yp"].transpose(1, 0, 2).reshape(D, T_P).T.reshape(2, 256, D)
        ng[2 * c:2 * c + 2] = r["ngla"].reshape(2, DEPTH, 2, 4, 64, 128)
        nh[2 * c:2 * c + 2] = r["nhg"]
    return yp, ys, ng, nh


def kernel(**inputs):
    if "nc" not in _CACHE:
        _CACHE["nc"] = build_nc()[0]
    nc = _CACHE["nc"]
    maps = make_in_maps(inputs)
    res = run_bass_kernel_spmd(nc, maps, core_ids=list(range(8)))
    return assemble(res.results)
```
